# Optimizing a Trainium2 kernel written in Bass

```python
import jax, jax.numpy as jnp
from jax import lax
import numpy as np

D_MODEL = 1024
BATCH = 8
SEQ = 4096
DEPTH = 2

GRID_W = 64
CTX_LEN = 256
N_HEADS = 8
N_KV_HEADS = 2
HEAD_DIM = 64
GQA_GROUP = N_HEADS // N_KV_HEADS
ATTN_WIDTH = N_HEADS * HEAD_DIM
KV_WIDTH = N_KV_HEADS * HEAD_DIM
Q_BLOCK = 128
ROPE_THETA = 10000.0
GMLP_GROUPS = 8
GMLP_GROUP_DIM = 64
GMLP_WIDTH = GMLP_GROUPS * GMLP_GROUP_DIM
CHUNK = 128
IN_PROJ_WIDTH = ATTN_WIDTH + 2 * KV_WIDTH + 2 * GMLP_WIDTH
MIX_WIDTH = ATTN_WIDTH + GMLP_WIDTH
POOL_WINDOWS = (2, 4, 8, 16)
POOL_GROUP_DIM = D_MODEL // len(POOL_WINDOWS)
D_FF = ((8 * D_MODEL // 3 + 255) // 256) * 256
N_EVEN = (DEPTH + 1) // 2
N_ODD = DEPTH // 2
EPS = 1e-6

kernel_name = "hybrid_gqa_gmlp_pool_dit_block"


def rms_norm(x, g):
    xf = x.astype(jnp.float32)
    y = xf * lax.rsqrt(jnp.mean(xf * xf, axis=-1, keepdims=True) + EPS)
    return (y * g.astype(jnp.float32)).astype(x.dtype)


def modulate(h, shift, scale):
    return h * (1.0 + scale) + shift


def adaln(cond, w_ada, b_ada):
    m = jax.nn.silu(cond) @ w_ada + b_ada
    return jnp.split(m, 6, axis=-1)


def axial_rope_tables(n):
    rows = n // GRID_W
    row = jnp.repeat(jnp.arange(rows), GRID_W).astype(jnp.float32)
    col = jnp.tile(jnp.arange(GRID_W), rows).astype(jnp.float32)
    half = HEAD_DIM // 2
    freqs = ROPE_THETA ** (-jnp.arange(0, half, 2, dtype=jnp.float32) / half)
    ang = jnp.concatenate([row[:, None] * freqs, col[:, None] * freqs], axis=-1)
    return jnp.cos(ang), jnp.sin(ang)


def apply_rope(x, cos, sin):
    xf = x.astype(jnp.float32)
    x1, x2 = xf[..., 0::2], xf[..., 1::2]
    c = cos[None, :, None, :]
    s = sin[None, :, None, :]
    out = jnp.stack([x1 * c - x2 * s, x1 * s + x2 * c], axis=-1).reshape(x.shape)
    return out.astype(x.dtype)


def attend(q, keys, vals):
    B, N = q.shape[0], q.shape[1]
    scale = HEAD_DIM ** -0.5
    qb = q.reshape(B, N // Q_BLOCK, Q_BLOCK, N_KV_HEADS, GQA_GROUP, HEAD_DIM).transpose(1, 0, 2, 3, 4, 5)

    def block(q_blk):
        s = jnp.einsum('bqkgd,bskd->bkgqs', q_blk, keys, preferred_element_type=jnp.float32) * scale
        p = jax.nn.softmax(s, axis=-1)
        return jnp.einsum('bkgqs,bskd->bqkgd', p.astype(vals.dtype), vals)

    out = lax.map(block, qb)
    return out.transpose(1, 0, 2, 3, 4, 5).reshape(B, N, ATTN_WIDTH)


def spatial_gating(u, v, g_v, w_s, b_s):
    B, N, _ = v.shape
    vg = rms_norm(v.reshape(B, N // CHUNK, CHUNK, GMLP_GROUPS, GMLP_GROUP_DIM), g_v)
    mixed = jnp.einsum('gpq,bnqgd->bnpgd', w_s, vg) + b_s.T[None, None, :, :, None]
    return u * mixed.reshape(B, N, GMLP_WIDTH)


def split_heads(t, h):
    return t.reshape(t.shape[0], t.shape[1], h, HEAD_DIM)


def attn_gmlp_mixer(xn, cn, w_in, w_out, q_norm, k_norm, g_v, w_s, b_s, cos, sin, ctx_live):
    splits = [ATTN_WIDTH, ATTN_WIDTH + KV_WIDTH, ATTN_WIDTH + 2 * KV_WIDTH,
              ATTN_WIDTH + 2 * KV_WIDTH + GMLP_WIDTH]
    qx, kx, vx, ux, gx = jnp.split(xn @ w_in, splits, axis=-1)
    qx = apply_rope(rms_norm(split_heads(qx, N_HEADS), q_norm), cos, sin)
    kx = apply_rope(rms_norm(split_heads(kx, N_KV_HEADS), k_norm), cos, sin)
    vx = split_heads(vx, N_KV_HEADS)
    if ctx_live:
        qc, kc, vc, uc, gc = jnp.split(cn @ w_in, splits, axis=-1)
    else:
        kc, vc = jnp.split(cn @ w_in[:, ATTN_WIDTH:ATTN_WIDTH + 2 * KV_WIDTH], [KV_WIDTH], axis=-1)
    kc = rms_norm(split_heads(kc, N_KV_HEADS), k_norm)
    vc = split_heads(vc, N_KV_HEADS)
    attn_x = attend(qx, jnp.concatenate([kx, kc], axis=1), jnp.concatenate([vx, vc], axis=1))
    gmlp_x = spatial_gating(jax.nn.gelu(ux, approximate=False), jax.nn.gelu(gx, approximate=False), g_v, w_s, b_s)
    out_x = jnp.concatenate([attn_x, gmlp_x], axis=-1) @ w_out
    out_c = None
    if ctx_live:
        qc = rms_norm(split_heads(qc, N_HEADS), q_norm)
        attn_c = attend(qc, kc, vc)
        gmlp_c = spatial_gating(jax.nn.gelu(uc, approximate=False), jax.nn.gelu(gc, approximate=False), g_v, w_s, b_s)
        out_c = jnp.concatenate([attn_c, gmlp_c], axis=-1) @ w_out
    return out_x, out_c


def multiscale_pool(h, w_pool, pool_scale):
    B, N, D = h.shape
    hf = h.astype(jnp.float32)
    cs = jnp.concatenate([jnp.zeros((B, 1, D), jnp.float32), jnp.cumsum(hf, axis=1)], axis=1)
    t = jnp.arange(N)
    outs = []
    for gi, w in enumerate(POOL_WINDOWS):
        left = w // 2
        right = w - 1 - left
        lo = jnp.clip(t - left, 0, N)
        hi = jnp.clip(t + right + 1, 0, N)
        sl = slice(gi * POOL_GROUP_DIM, (gi + 1) * POOL_GROUP_DIM)
        csg = cs[..., sl]
        cnt = (hi - lo).astype(jnp.float32)[None, :, None]
        mean = (jnp.take(csg, hi, axis=1) - jnp.take(csg, lo, axis=1)) / cnt
        outs.append(mean - hf[..., sl])
    pooled = jnp.stack(outs, axis=2).astype(h.dtype)
    y = jnp.einsum('bngc,gcd->bngd', pooled, w_pool).reshape(B, N, D)
    return y * pool_scale


def swiglu(h, w1, w3, w2):
    return (jax.nn.silu(h @ w1) * (h @ w3)) @ w2


def setup_inputs(seed: int = 0) -> dict:
    key = jax.random.key(seed)
    ks = iter(jax.random.split(key, 32))
    f32 = jnp.float32

    def nrm(shape, scale):
        return jax.random.normal(next(ks), shape, f32) * scale

    D = D_MODEL
    return {
        "x": nrm((BATCH, SEQ, D), 1.0),
        "c": nrm((BATCH, D), 1.0),
        "ctx": nrm((BATCH, CTX_LEN, D), 1.0),
        "c_ctx": nrm((D,), 1.0),
        "w_ada": nrm((DEPTH, D, 6 * D), 0.5 * D ** -0.5),
        "b_ada": nrm((DEPTH, 6 * D), 0.02),
        "g_mix": 1.0 + nrm((DEPTH, D), 0.05),
        "g_ffn": 1.0 + nrm((DEPTH, D), 0.05),
        "w_in": nrm((N_EVEN, D, IN_PROJ_WIDTH), D ** -0.5),
        "w_out": nrm((N_EVEN, MIX_WIDTH, D), MIX_WIDTH ** -0.5),
        "q_norm": 1.0 + nrm((N_EVEN, HEAD_DIM), 0.05),
        "k_norm": 1.0 + nrm((N_EVEN, HEAD_DIM), 0.05),
        "gmlp_norm": 1.0 + nrm((N_EVEN, GMLP_GROUPS, GMLP_GROUP_DIM), 0.05),
        "w_spatial": nrm((N_EVEN, GMLP_GROUPS, CHUNK, CHUNK), 0.5 * CHUNK ** -0.5),
        "b_spatial": 1.0 + nrm((N_EVEN, GMLP_GROUPS, CHUNK), 0.1),
        "w_pool": nrm((N_ODD, len(POOL_WINDOWS), POOL_GROUP_DIM, POOL_GROUP_DIM), POOL_GROUP_DIM ** -0.5),
        "pool_scale": 1.0 + nrm((N_ODD, D), 0.1),
        "w1": nrm((DEPTH, D, D_FF), D ** -0.5),
        "w3": nrm((DEPTH, D, D_FF), D ** -0.5),
        "w2": nrm((DEPTH, D_FF, D), D_FF ** -0.5),
        "g_final": 1.0 + nrm((D,), 0.05),
    }


def reference(x, c, ctx, c_ctx, w_ada, b_ada, g_mix, g_ffn, w_in, w_out, q_norm, k_norm,
              gmlp_norm, w_spatial, b_spatial, w_pool, pool_scale, w1, w3, w2, g_final):
    S = x.shape[1]
    cos, sin = axial_rope_tables(S)
    h_ctx = ctx
    for i in range(DEPTH):
        ctx_live = any(j % 2 == 0 for j in range(i + 1, DEPTH))
        sh1, sc1, ga1, sh2, sc2, ga2 = [m[:, None, :] for m in adaln(c, w_ada[i], b_ada[i])]
        csh1, csc1, cga1, csh2, csc2, cga2 = adaln(c_ctx, w_ada[i], b_ada[i])
        xn = modulate(rms_norm(x, g_mix[i]), sh1, sc1)
        if i % 2 == 0:
            e = i // 2
            cn = modulate(rms_norm(h_ctx, g_mix[i]), csh1, csc1)
            mix_x, mix_c = attn_gmlp_mixer(xn, cn, w_in[e], w_out[e], q_norm[e], k_norm[e], gmlp_norm[e],
                                           w_spatial[e], b_spatial[e], cos, sin, ctx_live)
        else:
            o = i // 2
            mix_x = multiscale_pool(xn, w_pool[o], pool_scale[o])
            mix_c = None
            if ctx_live:
                cn = modulate(rms_norm(h_ctx, g_mix[i]), csh1, csc1)
                mix_c = multiscale_pool(cn, w_pool[o], pool_scale[o])
        x = x + ga1 * mix_x
        x = x + ga2 * swiglu(modulate(rms_norm(x, g_ffn[i]), sh2, sc2), w1[i], w3[i], w2[i])
        if ctx_live:
            h_ctx = h_ctx + cga1 * mix_c
            h_ctx = h_ctx + cga2 * swiglu(modulate(rms_norm(h_ctx, g_ffn[i]), csh2, csc2), w1[i], w3[i], w2[i])
    return rms_norm(x, g_final)
```

```python
import contextlib
import os
import numpy as np
import ml_dtypes
import concourse.bass as bass
import concourse.mybir as mybir
from concourse.bass_utils import run_bass_kernel_spmd

F32 = mybir.dt.float32
BF16 = mybir.dt.bfloat16
ALU = mybir.AluOpType
AF = mybir.ActivationFunctionType
AX = mybir.AxisListType

D = 1024
S = 4096
NT = 32
NTC = 34
DFF = 2816
NF = 22
EPS = 1e-6
SB_BASE = 16512
SB_LIMIT = 229376


class Buf:
    def __init__(self, ap, name, space=None, lo=0, hi=0):
        self.ap = ap
        self.name = name
        self.space = space
        self.lo = lo
        self.hi = hi
        self.w = None
        self.r = {}
        self.dsem = None
        self.dead = False

    def __getitem__(self, k):
        return self.ap[k]


class DSem:
    def __init__(self):
        self.h = None
        self.count = 0


class Ins:
    __slots__ = ("eng", "fn", "deps", "signal", "semval", "dsem", "idx")

    def __init__(self, eng, fn, dsem, idx):
        self.eng = eng
        self.fn = fn
        self.deps = []
        self.signal = False
        self.semval = 0
        self.dsem = dsem
        self.idx = idx


ENGS = ("pe", "dve", "act", "pool", "sp")


class Prog:
    def __init__(self, nc):
        self.nc = nc
        self.q = {e: [] for e in ENGS}
        self.n = 0
        self.dsems = []
        self.live = {"sb": [], "ps": []}
        self.sb_off = SB_BASE
        self.nalloc = 0

    def _register(self, b):
        if b.space is None:
            return
        keep = []
        for o in self.live[b.space]:
            if o.lo < b.hi and b.lo < o.hi:
                o.dead = True
                if o.w is not None:
                    b.r[("w", id(o.w))] = o.w
                for k, v in o.r.items():
                    b.r[(k, id(v))] = v
                if not (b.lo <= o.lo and o.hi <= b.hi):
                    keep.append(o)
            else:
                keep.append(o)
        keep.append(b)
        self.live[b.space] = keep

    def sb(self, name, shape, dt, off=None):
        esz = 2 if dt == BF16 else 4
        n = 1
        for s in shape[1:]:
            n *= s
        nbytes = (n * esz + 63) // 64 * 64
        if off is None:
            off = self.sb_off
            self.sb_off += nbytes
        assert off + nbytes <= SB_LIMIT, (name, off, nbytes)
        self.nalloc += 1
        t = self.nc.alloc_sbuf_tensor_at("%s_%d" % (name, self.nalloc), list(shape), dt, offset=off)
        b = Buf(t, name, "sb", off, off + nbytes)
        self._register(b)
        return b

    def view(self, ap, name, space, lo, hi):
        b = Buf(ap, name, space, lo, hi)
        self._register(b)
        return b

    def op(self, eng, fn, reads=(), writes=(), dma=None):
        dsem = None
        if dma is not None:
            if dma.dsem is None:
                dma.dsem = DSem()
                self.dsems.append(dma.dsem)
            dsem = dma.dsem
        ins = Ins(eng, fn, dsem, self.n)
        self.n += 1
        deps = {}
        for b in reads:
            assert not b.dead, b.name
            if b.w is not None:
                deps[id(b.w)] = b.w
        for b in writes:
            assert not b.dead, b.name
            if b.w is not None:
                deps[id(b.w)] = b.w
            for r in b.r.values():
                deps[id(r)] = r
        ins.deps = list(deps.values())
        key = eng if dsem is None else ("dma", ins.idx)
        for b in reads:
            b.r[key] = ins
        for b in writes:
            b.w = ins
            b.r = {}
        self.q[eng].append(ins)
        return ins

    def emit(self, final_waits=()):
        nc = self.nc
        for e in ENGS:
            for ins in self.q[e]:
                nd = []
                for d in ins.deps:
                    if d is ins:
                        continue
                    if d.dsem is None and ins.dsem is None and d.eng == "pe" and ins.eng == "pe":
                        continue
                    if d.dsem is not None and ins.dsem is d.dsem:
                        continue
                    nd.append(d)
                    d.signal = True
                ins.deps = nd
        for d in final_waits:
            d.signal = True
        cnt = {e: 0 for e in ENGS}
        allins = sorted((i for e in ENGS for i in self.q[e]), key=lambda i: i.idx)
        for ins in allins:
            if ins.dsem is not None:
                ins.dsem.count += 16
                ins.semval = ins.dsem.count
            elif ins.signal:
                cnt[ins.eng] += 1
                ins.semval = cnt[ins.eng]
        with contextlib.ExitStack() as st:
            esem = {e: st.enter_context(nc.semaphore("s_" + e)) for e in ENGS}
            for i, ds in enumerate(self.dsems):
                ds.h = st.enter_context(nc.semaphore("d%d" % i))
            block = st.enter_context(nc.Block())

            def run(e, eh):
                waited = {}
                for ins in self.q[e]:
                    need = {}
                    for d in ins.deps:
                        h = d.dsem.h if d.dsem is not None else esem[d.eng]
                        k = id(h)
                        if need.get(k, (None, 0))[1] < d.semval:
                            need[k] = (h, d.semval)
                    for k, (h, v) in need.items():
                        if waited.get(k, 0) < v:
                            eh.wait_ge(h, v)
                            waited[k] = v
                    r = ins.fn(eh)
                    if ins.dsem is not None:
                        r.then_inc(ins.dsem.h, 16)
                    elif ins.signal:
                        r.then_inc(esem[e], 1)
                if e == "sp":
                    for d in final_waits:
                        h = d.dsem.h if d.dsem is not None else esem[d.eng]
                        eh.wait_ge(h, d.semval)

            block.tensor(lambda eh: run("pe", eh))
            block.vector(lambda eh: run("dve", eh))
            block.scalar(lambda eh: run("act", eh))
            block.gpsimd(lambda eh: run("pool", eh))
            block.sync(lambda eh: run("sp", eh))


def build_program(debug=None, upto=99):
    nc = bass.Bass("TRN2", target_bir_lowering=False)
    P = Prog(nc)

    def din(name, shape, dt=F32):
        return nc.dram_tensor(name, list(shape), dt, kind="ExternalInput").ap()

    def dscr(name, shape, dt=F32):
        return nc.dram_tensor(name, list(shape), dt, kind="Internal").ap()

    x_d = din("x", [S, D])
    ctx_d = din("ctx", [256, D])
    cT_d = din("cT", [128, 8, 2])
    wada_d = din("w_ada", [2, D, 6 * D])
    bada_d = din("b_ada", [2, 6 * D])
    gmix_d = din("g_mix", [2, D])
    gffn_d = din("g_ffn", [2, D])
    gfin_d = din("g_final", [D])
    win_d = din("w_in", [D, 1792])
    wout_d = din("w_out", [D, D])
    gqk_d = din("gqk", [640])
    gv_d = din("gv", [512])
    wsT_d = din("wsT", [128, 8, 128])
    bsT_d = din("bsT", [128, 8])
    wpool_d = din("w_pool", [4, 256, 256])
    pscale_d = din("pool_scale", [D])
    w1_d = din("w1", [2, D, DFF])
    w3_d = din("w3", [2, D, DFF])
    w2_d = din("w2", [2, DFF, D])
    ident_d = din("ident", [128, 128], BF16)
    rope_d = din("rope", [128, NTC, 128])
    band_d = din("band", [128, 4, 5, 128], BF16)
    out_d = nc.dram_tensor("out", [S, D], F32, kind="ExternalOutput").ap()

    vec_d = dscr("vecscr", [2, 2, 9 * D])
    qkind = "ExternalOutput" if (debug and "qt" in debug) else "Internal"
    qt_d = nc.dram_tensor("qt_scr", [NT, 128, 512], BF16, kind=qkind).ap()
    gmt_d = nc.dram_tensor("gmt_scr", [NT, 128, 512], BF16, kind=qkind).ap()
    dram_qt = Buf(None, "qt_scr")
    dram_gmt = Buf(None, "gmt_scr")
    xa_d = nc.dram_tensor("xa", [S, D], F32, kind=("ExternalOutput" if (debug and "x1" in debug) else "Internal")).ap()
    xb_d = dscr("xb", [S, D])
    dbg = {}
    if debug:
        for name, shape in debug.items():
            if not isinstance(shape, (list, tuple)) or len(shape) < 2:
                continue
            dbg[name] = nc.dram_tensor("dbg_" + name, list(shape), F32, kind="ExternalOutput").ap()
    dram_vec = Buf(None, "vecscr")
    dram_xa = Buf(None, "xa")
    dram_xb = Buf(None, "xb")
    finals = []

    psA = nc.alloc_psum_tensor("psA", [128, 2048], F32)
    psB = nc.alloc_psum_tensor("psB", [128, 2048], F32)

    def bank(i, name, ncols=512, dt=F32, nb=1):
        t = psA if i < 4 else psB
        j = i % 4
        ap = t[:, j * 512:(j + nb) * 512]
        if dt == BF16:
            ap = ap.bitcast(BF16)
        return P.view(ap, name, "ps", i, i + nb)

    ident = P.sb("ident", [128, 128], BF16)
    epsT = P.sb("epsT", [128, 16], F32)
    nhT = P.sb("nhT", [128, 16], F32)
    onesF = P.sb("onesF", [128, 64], F32)
    P.op("sp", lambda e: e.dma_start(out=ident[:], in_=ident_d[:, :]), writes=[ident], dma=ident)
    P.op("pool", lambda e: e.memset(epsT[:], EPS), writes=[epsT])
    P.op("pool", lambda e: e.memset(nhT[:], -0.5), writes=[nhT])
    P.op("pool", lambda e: e.memset(onesF[:], 1.0), writes=[onesF])
    negC = P.sb("negC", [128, 1], F32)
    mqk = P.sb("mqk", [128, 2], F32)
    persist_mark = P.sb_off

    def dbg_dump(name, buf, ap_fn, dram_ap_fn):
        if name in dbg:
            i = P.op("sp", lambda e: e.dma_start(out=dram_ap_fn(dbg[name]), in_=ap_fn(buf)), reads=[buf], dma=buf)
            finals.append(i)

    def rstd_pool(ms, rs, n):
        P.op("pool", lambda e: e.tensor_tensor(out=rs[:, 0:n], in0=ms[:, 0:n], in1=epsT[:, 0:n], op=ALU.add),
             reads=[ms, epsT], writes=[rs])
        P.op("pool", lambda e: e.tensor_tensor(out=rs[:, 0:n], in0=rs[:, 0:n], in1=nhT[:, 0:n], op=ALU.pow),
             reads=[rs, nhT], writes=[rs])

    P.sb_off = persist_mark
    cT = P.sb("cT", [128, 8, 2], F32)
    scT = P.sb("scT", [128, 8, 2], F32)
    modrow = [P.sb("modrow%d" % l, [2, 9 * D], F32) for l in range(2)]
    brow = [P.sb("brow%d" % l, [2, 6 * D], F32) for l in range(2)]
    grow = P.sb("grow", [2, 5, D], F32)
    wa = [P.sb("wa%d" % i, [128, 8, 512], F32) for i in range(2)]
    pm0 = [bank(i, "pm0_%d" % i) for i in range(2)]

    P.op("sp", lambda e: e.dma_start(out=cT[:], in_=cT_d[:, :, :]), writes=[cT], dma=cT)
    P.op("act", lambda e: e.activation(out=scT[:], in_=cT[:], func=AF.Silu), reads=[cT], writes=[scT])
    for l in range(2):
        P.op("sp", lambda e, l=l: e.dma_start(out=brow[l][:], in_=bada_d[l].partition_broadcast(2)),
             writes=[brow[l]], dma=brow[l])
    P.op("sp", lambda e: e.dma_start(out=grow[:, 0:2, :], in_=gmix_d.partition_broadcast(2)), writes=[grow], dma=grow)
    P.op("sp", lambda e: e.dma_start(out=grow[:, 2:4, :], in_=gffn_d.partition_broadcast(2)), writes=[grow], dma=grow)
    P.op("sp", lambda e: e.dma_start(out=grow[:, 4, :], in_=pscale_d.partition_broadcast(2)), writes=[grow], dma=grow)
    it = 0
    for l in range(2):
        for cb in range(12):
            w = wa[it % 2]
            pm = pm0[it % 2]
            P.op(os.environ.get("WAQ", "sp" if it % 2 == 0 else "pool") if os.environ.get("WAQ") else ("sp" if it % 2 == 0 else "pool"),
                 lambda e, l=l, cb=cb, w=w: e.dma_start(
                     out=w[:], in_=wada_d[l, :, cb * 512:(cb + 1) * 512].rearrange("(k p) n -> p k n", p=128)),
                 writes=[w], dma=w)
            for k in range(8):
                P.op("pe", lambda e, k=k, w=w, pm=pm: e.matmul(pm[0:2, :], lhsT=scT[:, k, :], rhs=w[:, k, :],
                                                                start=(k == 0), stop=(k == 7)),
                     reads=[scT, w], writes=[pm])
            P.op("dve", lambda e, l=l, cb=cb, pm=pm: e.tensor_tensor(
                out=modrow[l][:, cb * 512:(cb + 1) * 512], in0=pm[0:2, :], in1=brow[l][:, cb * 512:(cb + 1) * 512],
                op=ALU.add), reads=[pm, brow[l]], writes=[modrow[l]])
            it += 1
        P.op("dve", lambda e, l=l: e.scalar_tensor_tensor(
            out=modrow[l][:, 6 * D:7 * D], in0=modrow[l][:, 1 * D:2 * D], scalar=1.0, in1=grow[:, l, :],
            op0=ALU.add, op1=ALU.mult), reads=[modrow[l], grow], writes=[modrow[l]])
        P.op("dve", lambda e, l=l: e.scalar_tensor_tensor(
            out=modrow[l][:, 7 * D:8 * D], in0=modrow[l][:, 4 * D:5 * D], scalar=1.0, in1=grow[:, 2 + l, :],
            op0=ALU.add, op1=ALU.mult), reads=[modrow[l], grow], writes=[modrow[l]])
        P.op("dve", lambda e, l=l: e.tensor_tensor(
            out=modrow[l][:, 8 * D:9 * D], in0=modrow[l][:, 2 * D:3 * D], in1=grow[:, 4, :], op=ALU.mult),
            reads=[modrow[l], grow], writes=[modrow[l]])
        P.op("sp", lambda e, l=l: e.dma_start(out=vec_d[l], in_=modrow[l][:]), reads=[modrow[l]], writes=[dram_vec],
             dma=modrow[l])

    def load_vec(buf, l, row, idx):
        P.op("sp", lambda e: e.dma_start(out=buf[:], in_=vec_d[l, row, idx * D:(idx + 1) * D].partition_broadcast(128)),
             reads=[dram_vec], writes=[buf], dma=buf)

    if "mod" in dbg:
        for l in range(2):
            i = P.op("sp", lambda e, l=l: e.dma_start(out=dbg["mod"][l], in_=modrow[l][:]), reads=[modrow[l]],
                     dma=modrow[l])
            finals.append(i)
    if upto <= 0:
        P.emit(finals)
        return nc

    P.sb_off = persist_mark
    KT = P.sb("KT", [128, NTC * 128], BF16)
    VA = P.sb("VA", [128, NTC, 2, 66], BF16)
    p2_mark = P.sb_off
    A1 = P.sb("A1", [128, D], F32)
    B1 = P.sb("B1", [128, D], F32)
    A1c = P.sb("A1c", [128, D], F32)
    B1c = P.sb("B1c", [128, D], F32)
    load_vec(A1, 0, 0, 6)
    load_vec(B1, 0, 0, 0)
    load_vec(A1c, 0, 1, 6)
    load_vec(B1c, 0, 1, 0)
    win = P.sb("win", [128, 8, 1792], BF16)
    for k in range(8):
        P.op("pool", lambda e, k=k: e.dma_start(out=win[:, k, :], in_=win_d[k * 128:(k + 1) * 128, :]),
             writes=[win], dma=win)
    wsT = P.sb("wsT", [128, 8, 128], BF16)
    P.op("pool", lambda e: e.dma_start(out=wsT[:], in_=wsT_d[:, :, :]), writes=[wsT], dma=wsT)
    bsT = P.sb("bsT", [128, 8], F32)
    P.op("sp", lambda e: e.dma_start(out=bsT[:], in_=bsT_d[:, :]), writes=[bsT], dma=bsT)
    gqk = P.sb("gqk", [128, 640], F32)
    P.op("sp", lambda e: e.dma_start(out=gqk[:], in_=gqk_d.partition_broadcast(128)), writes=[gqk], dma=gqk)
    P.op("dve", lambda e: e.tensor_reduce(out=mqk[:, 0:1], in_=gqk[:, 0:64], axis=AX.X, op=ALU.max,
                                          apply_absolute_value=True), reads=[gqk], writes=[mqk])
    P.op("dve", lambda e: e.tensor_reduce(out=mqk[:, 1:2], in_=gqk[:, 512:576], axis=AX.X, op=ALU.max,
                                          apply_absolute_value=True), reads=[gqk], writes=[mqk])
    P.op("dve", lambda e: e.scalar_tensor_tensor(out=negC[:], in0=mqk[:, 0:1], scalar=-8.0, in1=mqk[:, 1:2],
                                                 op0=ALU.mult, op1=ALU.mult), reads=[mqk], writes=[negC])
    gv = P.sb("gv", [128, 512], F32)
    P.op("sp", lambda e: e.dma_start(out=gv[:], in_=gv_d.partition_broadcast(128)), writes=[gv], dma=gv)
    rope = P.sb("rope", [128, NTC, 128], F32)
    P.op("sp", lambda e: e.dma_start(out=rope[:], in_=rope_d[:, :, :]), writes=[rope], dma=rope)
    P.op("dve", lambda e: e.memset(VA[:, :, :, 64:66], 1.0), writes=[VA])

    NB = 2
    xt = [P.sb("xt%d" % i, [128, D], F32) for i in range(NB)]
    junk = [P.sb("junk%d" % i, [128, D], BF16) for i in range(2)]
    ms = [P.sb("ms%d" % i, [128, 1], F32) for i in range(NB)]
    rs = [P.sb("rs%d" % i, [128, 1], F32) for i in range(NB)]
    xm = xt
    qts = [P.sb("qts%d" % i, [128, 512], BF16) for i in range(NB)]
    gts = [P.sb("gts%d" % i, [128, 512], BF16) for i in range(NB)]
    xn = [P.sb("xn%d" % i, [128, D], BF16) for i in range(NB)]
    xnT = [P.sb("xnT%d" % i, [128, 8, 128], BF16) for i in range(NB)]
    qsq = [P.sb("qsq%d" % i, [128, 640], F32) for i in range(NB)]
    msq = [P.sb("msq%d" % i, [128, 10], F32) for i in range(NB)]
    rq = [P.sb("rq%d" % i, [128, 10], F32) for i in range(NB)]
    qn = [P.sb("qn%d" % i, [128, 640], F32) for i in range(NB)]
    t1 = [P.sb("t1%d" % i, [128, 640], F32) for i in range(NB)]
    t2 = [P.sb("t2%d" % i, [128, 640], F32) for i in range(NB)]
    qr = [P.sb("qr%d" % i, [128, 640], BF16) for i in range(NB)]
    ug = [P.sb("ug%d" % i, [128, 512], F32) for i in range(NB)]
    vg = [P.sb("vg%d" % i, [128, 512], F32) for i in range(NB)]
    msv = [P.sb("msv%d" % i, [128, 8], F32) for i in range(NB)]
    rv = [P.sb("rv%d" % i, [128, 8], F32) for i in range(NB)]
    vn = [P.sb("vn%d" % i, [128, 512], F32) for i in range(NB)]
    vnb = [P.sb("vnb%d" % i, [128, 512], BF16) for i in range(NB)]
    m1 = [P.sb("m1%d" % i, [128, 512], F32) for i in range(NB)]
    gm = [P.sb("gm%d" % i, [128, 512], BF16) for i in range(NB)]
    pT = bank(0, "pT", dt=BF16)
    pqk = bank(1, "pqk", nb=2)
    pu = bank(3, "pu")
    pg = bank(4, "pg")
    pmx = bank(5, "pmx")
    pT2 = bank(6, "pT2", dt=BF16)
    pT3 = bank(7, "pT3", dt=BF16)

    for t in range(NTC):
        s = t % NB
        isx = t < NT
        src = x_d[t * 128:(t + 1) * 128, :] if isx else ctx_d[(t - NT) * 128:(t - NT + 1) * 128, :]
        Ag, Bg = (A1, B1) if isx else (A1c, B1c)
        P.op("sp", lambda e, s=s, src=src: e.dma_start(out=xt[s][:], in_=src), writes=[xt[s]], dma=xt[s])
        P.op("act", lambda e, s=s: e.activation(out=junk[0][:], in_=xt[s][:], func=AF.Square, scale=1.0 / 32.0,
                                                accum_out=ms[s][:]), reads=[xt[s]], writes=[junk[0], ms[s]])
        rstd_pool(ms[s], rs[s], 1)
        P.op("dve", lambda e, s=s, Ag=Ag: e.scalar_tensor_tensor(out=xm[s][:], in0=xt[s][:], scalar=rs[s][:, 0:1],
                                                                  in1=Ag[:], op0=ALU.mult, op1=ALU.mult),
             reads=[xt[s], rs[s], Ag], writes=[xm[s]])
        P.op("pool", lambda e, s=s, Bg=Bg: e.tensor_tensor(out=xn[s][:], in0=xm[s][:], in1=Bg[:], op=ALU.add),
             reads=[xm[s], Bg], writes=[xn[s]])
        for k in range(8):
            P.op("pe", lambda e, s=s, k=k: e.transpose(out=pT[:, k * 128:(k + 1) * 128],
                                                        in_=xn[s][:, k * 128:(k + 1) * 128], identity=ident[:]),
                 reads=[xn[s], ident], writes=[pT])
        P.op("act", lambda e, s=s: e.copy(out=xnT[s][:].rearrange("p k t -> p (k t)"), in_=pT[:, :]),
             reads=[pT], writes=[xnT[s]])
        blocks = [(pqk, 0, 0, 512), (pqk, 512, 512, 256)]
        if isx:
            blocks += [(pu, 0, 768, 512), (pg, 0, 1280, 512)]
        else:
            blocks = [(pqk, 512, 512, 256)]
        for (pb, po, wo, n) in blocks:
            for k in range(8):
                P.op("pe", lambda e, s=s, k=k, pb=pb, po=po, wo=wo, n=n: e.matmul(
                    pb[:, po:po + n], lhsT=xnT[s][:, k, :], rhs=win[:, k, wo:wo + n], start=(k == 0), stop=(k == 7)),
                    reads=[xnT[s], win], writes=[pb])
        c0 = 0 if isx else 512
        nh = 10 if isx else 2
        h0 = 0 if isx else 8
        w_ = nh * 64
        P.op("act", lambda e, s=s, c0=c0, w_=w_: e.activation(out=qsq[s][:, c0:c0 + w_], in_=pqk[:, c0:c0 + w_],
                                                             func=AF.Square, scale=0.125),
             reads=[pqk], writes=[qsq[s]])
        P.op("dve", lambda e, s=s, c0=c0, w_=w_, nh=nh, h0=h0: e.tensor_reduce(
            out=msq[s][:, h0:h0 + nh], in_=qsq[s][:, c0:c0 + w_].rearrange("p (h d) -> p h d", d=64), axis=AX.X,
            op=ALU.add), reads=[qsq[s]], writes=[msq[s]])
        rstd_pool(msq[s], rq[s], 10)
        P.op("dve", lambda e, s=s, c0=c0, w_=w_, nh=nh, h0=h0: e.tensor_tensor(
            out=qn[s][:, c0:c0 + w_].rearrange("p (h d) -> p h d", d=64),
            in0=pqk[:, c0:c0 + w_].rearrange("p (h d) -> p h d", d=64),
            in1=rq[s][:, h0:h0 + nh].unsqueeze(2).to_broadcast([128, nh, 64]), op=ALU.mult),
            reads=[pqk, rq[s]], writes=[qn[s]])
        P.op("pool", lambda e, s=s, c0=c0, w_=w_: e.tensor_tensor(out=qn[s][:, c0:c0 + w_], in0=qn[s][:, c0:c0 + w_],
                                                                 in1=gqk[:, c0:c0 + w_], op=ALU.mult),
             reads=[qn[s], gqk], writes=[qn[s]])
        cosd = rope[:, t, 0:64]
        sinsg = rope[:, t, 64:128]
        P.op("dve", lambda e, s=s, c0=c0, w_=w_, nh=nh, cosd=cosd: e.tensor_tensor(
            out=t1[s][:, c0:c0 + w_].rearrange("p (h d) -> p h d", d=64),
            in0=qn[s][:, c0:c0 + w_].rearrange("p (h d) -> p h d", d=64),
            in1=cosd.unsqueeze(1).to_broadcast([128, nh, 64]), op=ALU.mult),
            reads=[qn[s], rope], writes=[t1[s]])
        for par in range(2):
            P.op("pool", lambda e, s=s, c0=c0, w_=w_, nh=nh, sinsg=sinsg, par=par: e.tensor_tensor(
                out=t2[s][:, c0:c0 + w_].rearrange("p (h i two) -> p h i two", i=32, two=2)[:, :, :, par],
                in0=qn[s][:, c0:c0 + w_].rearrange("p (h i two) -> p h i two", i=32, two=2)[:, :, :, 1 - par],
                in1=sinsg.rearrange("p (i two) -> p i two", two=2)[:, :, par].unsqueeze(1).to_broadcast([128, nh, 32]),
                op=ALU.mult), reads=[qn[s], rope], writes=[t2[s]])
        P.op("dve", lambda e, s=s, c0=c0, w_=w_: e.tensor_tensor(out=qr[s][:, c0:c0 + w_], in0=t1[s][:, c0:c0 + w_],
                                                                in1=t2[s][:, c0:c0 + w_], op=ALU.add),
             reads=[t1[s], t2[s]], writes=[qr[s]])
        trs = list(range(5)) if isx else [4]
        for j in trs:
            P.op("pe", lambda e, s=s, j=j: e.transpose(out=pT2[:, j * 128:(j + 1) * 128],
                                                        in_=qr[s][:, j * 128:(j + 1) * 128], identity=ident[:]),
                 reads=[qr[s], ident], writes=[pT2])
        if isx:
            P.op("act", lambda e, s=s: e.copy(out=qts[s][:], in_=pT2[:, 0:512]), reads=[pT2], writes=[qts[s]])
            P.op("sp", lambda e, s=s, t=t: e.dma_start(out=qt_d[t], in_=qts[s][:]), reads=[qts[s]], writes=[dram_qt],
                 dma=qts[s])
        P.op("act", lambda e, t=t: e.copy(out=KT[:, t * 128:(t + 1) * 128], in_=pT2[:, 512:640]),
             reads=[pT2], writes=[KT])
        P.op("dve", lambda e, t=t: e.tensor_copy(out=VA[:, t, :, 0:64],
                                                 in_=pqk[:, 640:768].rearrange("p (h d) -> p h d", d=64)),
             reads=[pqk], writes=[VA])
        if not isx:
            continue
        P.op("act", lambda e, s=s: e.activation(out=ug[s][:], in_=pu[:, :], func=AF.Gelu), reads=[pu], writes=[ug[s]])
        P.op("act", lambda e, s=s: e.activation(out=vg[s][:], in_=pg[:, :], func=AF.Gelu), reads=[pg], writes=[vg[s]])
        P.op("act", lambda e, s=s: e.activation(out=junk[1][:, 0:512], in_=vg[s][:], func=AF.Square, scale=0.125),
             reads=[vg[s]], writes=[junk[1]])
        P.op("dve", lambda e, s=s: e.tensor_reduce(out=msv[s][:], in_=junk[1][:, 0:512].rearrange("p (h d) -> p h d", d=64),
                                                   axis=AX.X, op=ALU.add), reads=[junk[1]], writes=[msv[s]])
        rstd_pool(msv[s], rv[s], 8)
        P.op("dve", lambda e, s=s: e.tensor_tensor(
            out=vn[s][:].rearrange("p (h d) -> p h d", d=64), in0=vg[s][:].rearrange("p (h d) -> p h d", d=64),
            in1=rv[s][:, :].unsqueeze(2).to_broadcast([128, 8, 64]), op=ALU.mult), reads=[vg[s], rv[s]], writes=[vn[s]])
        P.op("pool", lambda e, s=s: e.tensor_tensor(out=vnb[s][:], in0=vn[s][:], in1=gv[:], op=ALU.mult),
             reads=[vn[s], gv], writes=[vnb[s]])
        for g in range(8):
            P.op("pe", lambda e, s=s, g=g: e.matmul(pmx[:, g * 64:(g + 1) * 64], lhsT=wsT[:, g, :],
                                                     rhs=vnb[s][:, g * 64:(g + 1) * 64], start=True, stop=True),
                 reads=[wsT, vnb[s]], writes=[pmx])
        P.op("dve", lambda e, s=s: e.tensor_tensor(
            out=m1[s][:].rearrange("p (h d) -> p h d", d=64), in0=pmx[:, :].rearrange("p (h d) -> p h d", d=64),
            in1=bsT[:, :].unsqueeze(2).to_broadcast([128, 8, 64]), op=ALU.add), reads=[pmx, bsT], writes=[m1[s]])
        P.op("pool", lambda e, s=s: e.tensor_tensor(out=gm[s][:], in0=m1[s][:], in1=ug[s][:], op=ALU.mult),
             reads=[m1[s], ug[s]], writes=[gm[s]])
        for j in range(4):
            P.op("pe", lambda e, s=s, j=j: e.transpose(out=pT3[:, j * 128:(j + 1) * 128],
                                                        in_=gm[s][:, j * 128:(j + 1) * 128], identity=ident[:]),
                 reads=[gm[s], ident], writes=[pT3])
        P.op("act", lambda e, s=s: e.copy(out=gts[s][:], in_=pT3[:, 0:512]), reads=[pT3], writes=[gts[s]])
        P.op("sp", lambda e, s=s, t=t: e.dma_start(out=gmt_d[t], in_=gts[s][:]), reads=[gts[s]], writes=[dram_gmt],
             dma=gts[s])

    if "kt" in dbg:
        stg = P.sb("stg", [128, NTC * 132], F32)
        P.op("dve", lambda e: e.tensor_copy(out=stg[:, 0:NTC * 128], in_=KT[:, :]), reads=[KT], writes=[stg])
        finals.append(P.op("sp", lambda e: e.dma_start(out=dbg["kt"][:, :], in_=stg[:, 0:NTC * 128]), reads=[stg], dma=stg))
        P.op("dve", lambda e: e.tensor_copy(out=stg[:, 0:NTC * 132].rearrange("p (c h d) -> p c h d", c=NTC, h=2),
                                            in_=VA[:, :, :, :]), reads=[VA], writes=[stg])
        finals.append(P.op("sp", lambda e: e.dma_start(out=dbg["va"][:, :], in_=stg[:, 0:NTC * 132]), reads=[stg], dma=stg))
    if upto <= 1:
        P.emit(finals)
        return nc

    P.sb_off = p2_mark
    G1 = P.sb("G1", [128, D], F32)
    load_vec(G1, 0, 0, 2)
    wo_a = P.sb("wo_a", [64, 8, D], BF16)
    wo_g = P.sb("wo_g", [128, 4, D], BF16)
    P.op("pool", lambda e: e.dma_start(out=wo_a[:], in_=wout_d[0:512, :].rearrange("(h d) n -> d h n", d=64)),
         writes=[wo_a], dma=wo_a)
    P.op("pool", lambda e: e.dma_start(out=wo_g[:], in_=wout_d[512:1024, :].rearrange("(f p) n -> p f n", p=128)),
         writes=[wo_g], dma=wo_g)
    qtile = [P.sb("qtile%d" % i, [128, 4, 512], BF16) for i in range(2)]
    gtile = [P.sb("gtile%d" % i, [128, 4, 512], BF16) for i in range(2)]
    PT = [P.sb("PT%d" % i, [128, 1024], BF16) for i in range(3)]
    AT = [P.sb("AT%d" % i, [64, 8, 512], BF16) for i in range(2)]
    rsb = [P.sb("rsb%d" % i, [128, 512], F32) for i in range(2)]
    bcs = [P.sb("bcs%d" % i, [64, 512], F32) for i in range(2)]
    xr = [P.sb("xr%d" % i, [128, D], F32) for i in range(2)]
    tmpb = [P.sb("tmpb%d" % i, [128, 512], F32) for i in range(2)]
    psS = [bank(0, "psS0", nb=2), bank(2, "psS1", nb=2)]
    psO = [bank(4, "psO0"), bank(5, "psO1")]
    psBc = bank(6, "psBc")
    psW = bank(7, "psW")

    NCG = NTC // 2
    groups = [(qt, hp, cg) for qt in range(8) for hp in range(8) for cg in range(NCG)]
    deferred = {}

    def defer(i, fn):
        deferred.setdefault(i, []).append(fn)

    def emit_S(i):
        qt, hp, cg = groups[i]
        j, half = hp // 2, hp % 2
        sb_ = psS[i % 2]
        qtl = qtile[qt % 2]
        for c in range(2):
            ch = 2 * cg + c
            P.op("pe", lambda e, c=c, ch=ch, sb_=sb_, qtl=qtl, j=j, half=half: e.matmul(
                sb_[:, c * 512:(c + 1) * 512], lhsT=KT[half * 64:(half + 1) * 64, ch * 128:(ch + 1) * 128],
                rhs=qtl[half * 64:(half + 1) * 64, :, j * 128:(j + 1) * 128], start=True, stop=True),
                reads=[KT, qtl], writes=[sb_])

    def emit_exp(i):
        sb_ = psS[i % 2]
        pt = PT[i % 3]
        P.op("act", lambda e, sb_=sb_, pt=pt: e.activation(out=pt[:], in_=sb_[:, :], func=AF.Exp, bias=negC[:, 0:1],
                                                           scale=0.125), reads=[sb_, negC], writes=[pt])

    def emit_PV(i):
        qt, hp, cg = groups[i]
        half = hp % 2
        pt = PT[i % 3]
        po = psO[hp % 2]
        for c in range(2):
            ch = 2 * cg + c
            P.op("pe", lambda e, c=c, ch=ch, pt=pt, po=po, half=half, cg=cg: e.matmul(
                po[0:65, :], lhsT=VA[:, ch, half, 0:65], rhs=pt[:, c * 512:(c + 1) * 512],
                start=(cg == 0 and c == 0), stop=(cg == NCG - 1 and c == 1)), reads=[VA, pt], writes=[po])

    def emit_head_final(qt, hp, i):
        po = psO[hp % 2]
        r_ = rsb[hp % 2]
        b_ = bcs[hp % 2]
        at = AT[qt % 2]
        P.op("dve", lambda e: e.reciprocal(out=r_[64:65, :], in_=po[64:65, :]), reads=[po], writes=[r_])

        def pe_part():
            P.op("pe", lambda e: e.matmul(psBc[0:64, :], lhsT=onesF[64:65, 0:64], rhs=r_[64:65, :], start=True, stop=True),
                 reads=[onesF, r_], writes=[psBc])
            P.op("act", lambda e: e.copy(out=b_[:], in_=psBc[0:64, :]), reads=[psBc], writes=[b_])
            P.op("dve", lambda e: e.tensor_tensor(out=at[:, hp, :], in0=po[0:64, :], in1=b_[:], op=ALU.mult),
                 reads=[po, b_], writes=[at])
        defer(i + 2, pe_part)

    def emit_wout_block(qt, st, oh):
        at = AT[qt % 2]
        gtl = gtile[qt % 2]
        t = qt * 4 + st
        x_ = xr[st % 2]
        tb = tmpb[oh]
        if oh == 0:
            P.op("sp", lambda e: e.dma_start(out=x_[:], in_=x_d[t * 128:(t + 1) * 128, :]), writes=[x_], dma=x_)
        for hp in range(8):
            P.op("pe", lambda e, hp=hp: e.matmul(psW[:, :], lhsT=at[:, hp, st * 128:(st + 1) * 128],
                                                  rhs=wo_a[:, hp, oh * 512:(oh + 1) * 512], start=(hp == 0), stop=False),
                 reads=[at, wo_a], writes=[psW])
        for f in range(4):
            P.op("pe", lambda e, f=f: e.matmul(psW[:, :], lhsT=gtl[:, st, f * 128:(f + 1) * 128],
                                                rhs=wo_g[:, f, oh * 512:(oh + 1) * 512], start=False, stop=(f == 3)),
                 reads=[gtl, wo_g], writes=[psW])
        P.op("dve", lambda e: e.tensor_tensor(out=tb[:], in0=psW[:, :], in1=G1[:, oh * 512:(oh + 1) * 512], op=ALU.mult),
             reads=[psW, G1], writes=[tb])
        P.op("pool", lambda e: e.tensor_tensor(out=x_[:, oh * 512:(oh + 1) * 512], in0=x_[:, oh * 512:(oh + 1) * 512],
                                               in1=tb[:], op=ALU.add), reads=[x_, tb], writes=[x_])
        if oh == 1:
            P.op("sp", lambda e: e.dma_start(out=xa_d[t * 128:(t + 1) * 128, :], in_=x_[:]), reads=[x_], writes=[dram_xa],
                 dma=x_)

    NG = len(groups)
    for i in range(NG + 24):
        if i < NG:
            qt, hp, cg = groups[i]
            if hp == 0 and cg == 0:
                ql, gl = qtile[qt % 2], gtile[qt % 2]
                P.op("sp", lambda e, qt=qt, ql=ql: e.dma_start(out=ql[:], in_=qt_d[qt * 4:(qt + 1) * 4].rearrange("t p c -> p t c")),
                     reads=[dram_qt], writes=[ql], dma=ql)
                P.op("sp", lambda e, qt=qt, gl=gl: e.dma_start(out=gl[:], in_=gmt_d[qt * 4:(qt + 1) * 4].rearrange("t p c -> p t c")),
                     reads=[dram_gmt], writes=[gl], dma=gl)
            if i == 0:
                emit_S(0)
            emit_exp(i)
            if i + 1 < NG:
                emit_S(i + 1)
            emit_PV(i)
            if cg == NCG - 1:
                emit_head_final(qt, hp, i)
                if hp == 7:
                    for b in range(8):
                        defer(i + 4 + 2 * b, (lambda qt=qt, b=b: emit_wout_block(qt, b // 2, b % 2)))
        for fn in deferred.pop(i, []):
            fn()
    assert not deferred

    if "x1" in dbg:
        pass
    if upto <= 2:
        P.emit(finals)
        return nc

    def ffn_phase(l, src_d, dram_src, dst_d, dram_dst, final):
        P.sb_off = persist_mark
        w1b = P.sb("w1b", [128, 8, DFF], BF16)
        w3b = P.sb("w3b", [128, 8, DFF], BF16)
        w2b = P.sb("w2b", [128, NF, D], BF16)
        for k in range(8):
            P.op("pool", lambda e, k=k: e.dma_start(out=w1b[:, k, :], in_=w1_d[l, k * 128:(k + 1) * 128, :]),
                 writes=[w1b], dma=w1b)
            P.op("pool", lambda e, k=k: e.dma_start(out=w3b[:, k, :], in_=w3_d[l, k * 128:(k + 1) * 128, :]),
                 writes=[w3b], dma=w3b)
        for h in range(2):
            P.op("pool", lambda e, h=h: e.dma_start(
                out=w2b[:, h * 11:(h + 1) * 11, :],
                in_=w2_d[l, h * 1408:(h + 1) * 1408, :].rearrange("(f p) n -> p f n", p=128)), writes=[w2b], dma=w2b)
        A2 = P.sb("A2", [128, D], F32)
        B2 = P.sb("B2", [128, D], F32)
        G2 = P.sb("G2", [128, D], F32)
        load_vec(A2, l, 0, 7)
        load_vec(B2, l, 0, 3)
        load_vec(G2, l, 0, 5)
        if final:
            GF = P.sb("GF", [128, D], F32)
            P.op("sp", lambda e: e.dma_start(out=GF[:], in_=gfin_d.partition_broadcast(128)), writes=[GF], dma=GF)
        xt_ = [P.sb("fxt%d" % i, [128, D], F32) for i in range(2)]
        xn_ = [P.sb("fxn%d" % i, [128, D], BF16) for i in range(2)]
        fj = P.sb("fjunk", [128, D], BF16)
        ms_ = [P.sb("fms%d" % i, [128, 1], F32) for i in range(2)]
        rs_ = [P.sb("frs%d" % i, [128, 1], F32) for i in range(2)]
        xT = P.sb("fxT", [128, 8, 512], BF16)
        act = P.sb("fact", [128, NF, 512], BF16)
        sgb = [P.sb("fsg%d" % i, [128, 512], BF16) for i in range(2)]
        xr_ = [P.sb("fxr%d" % i, [128, D], F32) for i in range(2)]
        tb_ = [P.sb("ftb%d" % i, [128, 512], F32) for i in range(2)]
        ps13 = [bank(0, "ps13_0", nb=2), bank(2, "ps13_1", nb=2)]
        psY = [bank(4 + i, "psY%d" % i) for i in range(4)]
        for pb in ps13:
            pb.bf = pb.ap.bitcast(BF16)
        NTT = S // 512

        def prologue(T):
            for st in range(4):
                t = T * 4 + st
                s_ = st % 2
                P.op("sp", lambda e, t=t, s_=s_: e.dma_start(out=xt_[s_][:], in_=src_d[t * 128:(t + 1) * 128, :]),
                     reads=[dram_src], writes=[xt_[s_]], dma=xt_[s_])
                P.op("act", lambda e, s_=s_: e.activation(out=fj[:], in_=xt_[s_][:], func=AF.Square, scale=1.0 / 32.0,
                                                         accum_out=ms_[s_][:]), reads=[xt_[s_]], writes=[fj, ms_[s_]])
                rstd_pool(ms_[s_], rs_[s_], 1)
                P.op("dve", lambda e, s_=s_: e.scalar_tensor_tensor(out=xt_[s_][:], in0=xt_[s_][:], scalar=rs_[s_][:, 0:1],
                                                                     in1=A2[:], op0=ALU.mult, op1=ALU.mult),
                     reads=[xt_[s_], rs_[s_], A2], writes=[xt_[s_]])
                P.op("pool", lambda e, s_=s_: e.tensor_tensor(out=xn_[s_][:], in0=xt_[s_][:], in1=B2[:], op=ALU.add),
                     reads=[xt_[s_], B2], writes=[xn_[s_]])
                pb = ps13[st // 2]
                o = (st % 2) * 1024
                for k in range(8):
                    P.op("pe", lambda e, s_=s_, k=k, pb=pb, o=o: e.transpose(
                        out=pb.bf[:, o + k * 128:o + (k + 1) * 128], in_=xn_[s_][:, k * 128:(k + 1) * 128],
                        identity=ident[:]), reads=[xn_[s_], ident], writes=[pb])
                P.op("act", lambda e, st=st, pb=pb, o=o: e.copy(
                    out=xT[:, :, st * 128:(st + 1) * 128], in_=pb.bf[:, o:o + 1024].rearrange("p (k t) -> p k t", k=8)),
                    reads=[pb], writes=[xT])

        def hidden(T):
            for f in range(NF):
                pb = ps13[f % 2]
                for (wb, o) in ((w1b, 0), (w3b, 512)):
                    for k in range(8):
                        P.op("pe", lambda e, wb=wb, o=o, k=k, pb=pb, f=f: e.matmul(
                            pb[:, o:o + 512], lhsT=wb[:, k, f * 128:(f + 1) * 128], rhs=xT[:, k, :],
                            start=(k == 0), stop=(k == 7)), reads=[wb, xT], writes=[pb])
                sg = sgb[f % 2]
                P.op("act", lambda e, pb=pb, sg=sg: e.activation(out=sg[:], in_=pb[:, 0:512], func=AF.Silu),
                     reads=[pb], writes=[sg])
                P.op("dve", lambda e, pb=pb, sg=sg, f=f: e.tensor_tensor(out=act[:, f, :], in0=pb[:, 512:1024], in1=sg[:],
                                                                         op=ALU.mult), reads=[pb, sg], writes=[act])

        ycount = [0]

        def output(T):
            for st in range(4):
                t = T * 4 + st
                x_ = xr_[st % 2]
                P.op("sp", lambda e, t=t, x_=x_: e.dma_start(out=x_[:], in_=src_d[t * 128:(t + 1) * 128, :]),
                     reads=[dram_src], writes=[x_], dma=x_)
                for oh in range(2):
                    py = psY[ycount[0] % 4]
                    ycount[0] += 1
                    tb = tb_[oh]
                    for f in range(NF):
                        P.op("pe", lambda e, f=f, py=py, st=st, oh=oh: e.matmul(
                            py[:, :], lhsT=act[:, f, st * 128:(st + 1) * 128], rhs=w2b[:, f, oh * 512:(oh + 1) * 512],
                            start=(f == 0), stop=(f == NF - 1)), reads=[act, w2b], writes=[py])
                    P.op("dve", lambda e, py=py, tb=tb, oh=oh: e.tensor_tensor(
                        out=tb[:], in0=py[:, :], in1=G2[:, oh * 512:(oh + 1) * 512], op=ALU.mult),
                        reads=[py, G2], writes=[tb])
                    P.op("pool", lambda e, x_=x_, tb=tb, oh=oh: e.tensor_tensor(
                        out=x_[:, oh * 512:(oh + 1) * 512], in0=x_[:, oh * 512:(oh + 1) * 512], in1=tb[:], op=ALU.add),
                        reads=[x_, tb], writes=[x_])
                if final:
                    s_ = st % 2
                    P.op("act", lambda e, x_=x_, s_=s_: e.activation(out=fj[:], in_=x_[:], func=AF.Square, scale=1.0 / 32.0,
                                                                     accum_out=ms_[s_][:]),
                         reads=[x_], writes=[fj, ms_[s_]])
                    rstd_pool(ms_[s_], rs_[s_], 1)
                    P.op("dve", lambda e, x_=x_, s_=s_: e.scalar_tensor_tensor(
                        out=x_[:], in0=x_[:], scalar=rs_[s_][:, 0:1], in1=GF[:], op0=ALU.mult, op1=ALU.mult),
                        reads=[x_, rs_[s_], GF], writes=[x_])
                i_ = P.op("sp", lambda e, t=t, x_=x_: e.dma_start(out=dst_d[t * 128:(t + 1) * 128, :], in_=x_[:]),
                          reads=[x_], writes=[dram_dst], dma=x_)
                if final:
                    finals.append(i_)

        prologue(0)
        for T in range(NTT):
            hidden(T)
            if T + 1 < NTT:
                prologue(T + 1)
            output(T)

    dram_out = Buf(None, "out")
    if upto <= 3:
        ffn_phase(0, xa_d, dram_xa, out_d, dram_out, False)
        finals.append(dram_out.w)
        P.emit(finals)
        return nc
    ffn_phase(0, xa_d, dram_xa, xb_d, dram_xb, False)

    P.sb_off = persist_mark + 3 * 8 * DFF * 2
    A1 = P.sb("pA1", [128, D], F32)
    B1 = P.sb("pB1", [128, D], F32)
    G1p = P.sb("pG1", [128, D], F32)
    load_vec(A1, 1, 0, 6)
    load_vec(B1, 1, 0, 0)
    load_vec(G1p, 1, 0, 8)
    bandb = P.sb("band", [128, 4, 5, 128], BF16)
    P.op("sp", lambda e: e.dma_start(out=bandb[:], in_=band_d[:, :, :, :]), writes=[bandb], dma=bandb)
    wp = P.sb("wp", [128, 4, 2, 256], BF16)
    for g in range(4):
        P.op("pool", lambda e, g=g: e.dma_start(out=wp[:, g, :, :], in_=wpool_d[g].rearrange("(cc p) d -> p cc d", p=128)),
             writes=[wp], dma=wp)
    pxt = [P.sb("pxt%d" % i, [128, D], F32) for i in range(2)]
    pxn = [P.sb("pxn%d" % i, [128, D], BF16) for i in range(4)]
    pj = P.sb("pjunk", [128, D], BF16)
    pms = [P.sb("pms%d" % i, [128, 1], F32) for i in range(2)]
    prs = [P.sb("prs%d" % i, [128, 1], F32) for i in range(2)]
    pTs = [P.sb("pTs%d" % i, [128, 8, 128], BF16) for i in range(2)]
    pxr = [P.sb("pxr%d" % i, [128, D], F32) for i in range(2)]
    ptb = [P.sb("ptb%d" % i, [128, D], F32) for i in range(2)]
    psP = [bank(0, "psP0", nb=2), bank(2, "psP1", nb=2)]
    psYp = [bank(4, "psYp0", nb=2), bank(6, "psYp1", nb=2)]

    def poolA(i):
        s_ = i % 2
        P.op("sp", lambda e: e.dma_start(out=pxt[s_][:], in_=xb_d[i * 128:(i + 1) * 128, :]), reads=[dram_xb],
             writes=[pxt[s_]], dma=pxt[s_])
        P.op("act", lambda e: e.activation(out=pj[:], in_=pxt[s_][:], func=AF.Square, scale=1.0 / 32.0,
                                           accum_out=pms[s_][:]), reads=[pxt[s_]], writes=[pj, pms[s_]])
        rstd_pool(pms[s_], prs[s_], 1)
        P.op("dve", lambda e: e.scalar_tensor_tensor(out=pxt[s_][:], in0=pxt[s_][:], scalar=prs[s_][:, 0:1], in1=A1[:],
                                                     op0=ALU.mult, op1=ALU.mult), reads=[pxt[s_], prs[s_], A1],
             writes=[pxt[s_]])
        P.op("pool", lambda e: e.tensor_tensor(out=pxn[i % 4][:], in0=pxt[s_][:], in1=B1[:], op=ALU.add),
             reads=[pxt[s_], B1], writes=[pxn[i % 4]])

    def poolB(i):
        pp = psP[i % 2]
        py = psYp[i % 2]
        pt_ = pTs[i % 2]
        x_ = pxr[i % 2]
        tb = ptb[i % 2]
        P.op("sp", lambda e: e.dma_start(out=x_[:], in_=xb_d[i * 128:(i + 1) * 128, :]), reads=[dram_xb], writes=[x_],
             dma=x_)
        srcs = []
        if i > 0:
            srcs.append((i - 1, 0))
        srcs.append((i, 3 if i == 0 else (4 if i == NT - 1 else 1)))
        if i < NT - 1:
            srcs.append((i + 1, 2))
        for cc in range(8):
            g = cc // 2
            for n_, (src, kind) in enumerate(srcs):
                xs = pxn[src % 4]
                P.op("pe", lambda e, cc=cc, g=g, xs=xs, kind=kind, n_=n_: e.matmul(
                    pp[:, cc * 128:(cc + 1) * 128], lhsT=xs[:, cc * 128:(cc + 1) * 128], rhs=bandb[:, g, kind, :],
                    start=(n_ == 0), stop=(n_ == len(srcs) - 1)), reads=[xs, bandb], writes=[pp])
        P.op("act", lambda e: e.copy(out=pt_[:].rearrange("p c t -> p (c t)"), in_=pp[:, :]), reads=[pp], writes=[pt_])
        for g in range(4):
            for c2 in range(2):
                P.op("pe", lambda e, g=g, c2=c2: e.matmul(py[:, g * 256:(g + 1) * 256], lhsT=pt_[:, 2 * g + c2, :],
                                                           rhs=wp[:, g, c2, :], start=(c2 == 0), stop=(c2 == 1)),
                     reads=[pt_, wp], writes=[py])
        P.op("dve", lambda e: e.tensor_tensor(out=tb[:], in0=py[:, :], in1=G1p[:], op=ALU.mult), reads=[py, G1p], writes=[tb])
        P.op("pool", lambda e: e.tensor_tensor(out=x_[:], in0=x_[:], in1=tb[:], op=ALU.add), reads=[x_, tb], writes=[x_])
        P.op("sp", lambda e: e.dma_start(out=xa_d[i * 128:(i + 1) * 128, :], in_=x_[:]), reads=[x_], writes=[dram_xa], dma=x_)

    poolA(0)
    for i in range(NT):
        if i + 1 < NT:
            poolA(i + 1)
        poolB(i)
    if upto <= 4:
        P.emit(finals)
        return nc

    ffn_phase(1, xa_d, dram_xa, out_d, dram_out, True)
    P.emit(finals)
    return nc


HEAD_ORDER = [0, 4, 1, 5, 2, 6, 3, 7]


def _consts():
    bf = ml_dtypes.bfloat16
    ident = np.eye(128, dtype=np.float32).astype(bf)
    n = np.arange(S)
    row = (n // 64).astype(np.float64)
    col = (n % 64).astype(np.float64)
    half = 32
    freqs = 10000.0 ** (-np.arange(0, half, 2, dtype=np.float64) / half)
    ang = np.concatenate([row[:, None] * freqs, col[:, None] * freqs], axis=-1).astype(np.float32)
    cos = np.cos(ang).astype(np.float32)
    sin = np.sin(ang).astype(np.float32)
    cosd = np.repeat(cos, 2, axis=1)
    sinsg = np.stack([-sin, sin], axis=-1).reshape(S, 64)
    tab = np.concatenate([cosd, sinsg], axis=1)
    ctab = np.concatenate([np.ones((256, 64), np.float32), np.zeros((256, 64), np.float32)], axis=1)
    tab = np.concatenate([tab, ctab], axis=0).reshape(NTC, 128, 128).transpose(1, 0, 2)
    rope = np.ascontiguousarray(tab, dtype=np.float32)
    band = np.zeros((128, 4, 5, 128), np.float32)
    for g, w in enumerate((2, 4, 8, 16)):
        left = w // 2
        right = w - 1 - left
        for kind in range(5):
            for tt in range(128):
                if kind == 3:
                    T = tt
                elif kind == 4:
                    T = S - 128 + tt
                else:
                    T = 2048 + tt
                lo = max(T - left, 0)
                hi = min(T + right + 1, S)
                cnt = hi - lo
                base = T - tt
                for src in range(lo, hi):
                    rel = src - base
                    if kind == 0 and -128 <= rel < 0:
                        band[rel + 128, g, 0, tt] += 1.0 / cnt
                    elif kind == 2 and 128 <= rel < 256:
                        band[rel - 128, g, 2, tt] += 1.0 / cnt
                    elif kind in (1, 3, 4) and 0 <= rel < 128:
                        band[rel, g, kind, tt] += 1.0 / cnt
                if kind in (1, 3, 4):
                    band[tt, g, kind, tt] -= 1.0
    return ident, rope, band.astype(bf)


def make_in_maps(inp):
    f = lambda a: np.ascontiguousarray(np.asarray(a), dtype=np.float32)
    ident, rope, band = _consts()
    w_in = f(inp["w_in"])[0]
    qcols = np.concatenate([np.arange(h * 64, (h + 1) * 64) for h in HEAD_ORDER])
    w_in_p = np.ascontiguousarray(np.concatenate([w_in[:, qcols], w_in[:, 512:]], axis=1))
    w_out = f(inp["w_out"])[0]
    w_out_p = np.ascontiguousarray(np.concatenate([w_out[qcols, :], w_out[512:, :]], axis=0))
    gqk = np.concatenate([np.tile(f(inp["q_norm"])[0], 8), np.tile(f(inp["k_norm"])[0], 2)])
    gv = f(inp["gmlp_norm"])[0].reshape(512)
    wsT = np.ascontiguousarray(f(inp["w_spatial"])[0].transpose(2, 0, 1))
    bsT = np.ascontiguousarray(f(inp["b_spatial"])[0].T)
    shared = dict(
        w_ada=f(inp["w_ada"]), b_ada=f(inp["b_ada"]), g_mix=f(inp["g_mix"]), g_ffn=f(inp["g_ffn"]),
        g_final=f(inp["g_final"]), w_in=w_in_p, w_out=w_out_p, gqk=np.ascontiguousarray(gqk),
        gv=np.ascontiguousarray(gv), wsT=wsT, bsT=bsT, w_pool=f(inp["w_pool"])[0],
        pool_scale=f(inp["pool_scale"])[0], w1=f(inp["w1"]), w3=f(inp["w3"]), w2=f(inp["w2"]),
        ident=ident, rope=rope, band=band)
    x = f(inp["x"])
    c = f(inp["c"])
    ctx = f(inp["ctx"])
    c_ctx = f(inp["c_ctx"])
    maps = []
    for b in range(8):
        cv = np.stack([c[b], c_ctx], axis=0)
        cT = np.ascontiguousarray(cv.reshape(2, 8, 128).transpose(2, 1, 0))
        m = dict(shared)
        m.update(x=x[b], ctx=ctx[b], cT=cT)
        maps.append(m)
    return maps


def kernel(**inputs):
    nc = build_program()
    maps = make_in_maps(inputs)
    res = run_bass_kernel_spmd(nc, maps, core_ids=list(range(8)))
    return np.stack([np.asarray(r["out"], dtype=np.float32) for r in res.results], axis=0)
```

```python
import contextlib
import os
import numpy as np
import ml_dtypes
import concourse.bass as bass
import concourse.mybir as mybir
from concourse.bass_utils import run_bass_kernel_spmd

F32 = mybir.dt.float32
BF16 = mybir.dt.bfloat16
ALU = mybir.AluOpType
AF = mybir.ActivationFunctionType
AX = mybir.AxisListType

D = 1024
S = 4096
NT = 32
NTC = 34
DFF = 2816
NF = 22
EPS = 1e-6
SB_BASE = 16512
SB_LIMIT = 229376


class Buf:
    def __init__(self, ap, name, space=None, lo=0, hi=0):
        self.ap = ap
        self.name = name
        self.space = space
        self.lo = lo
        self.hi = hi
        self.w = None
        self.r = {}
        self.dsem = None
        self.dead = False

    def __getitem__(self, k):
        return self.ap[k]


class DSem:
    def __init__(self):
        self.h = None
        self.count = 0


class Ins:
    __slots__ = ("eng", "fn", "deps", "signal", "semval", "dsem", "idx")

    def __init__(self, eng, fn, dsem, idx):
        self.eng = eng
        self.fn = fn
        self.deps = []
        self.signal = False
        self.semval = 0
        self.dsem = dsem
        self.idx = idx


ENGS = ("pe", "dve", "act", "pool", "sp")


class Prog:
    def __init__(self, nc):
        self.nc = nc
        self.q = {e: [] for e in ENGS}
        self.n = 0
        self.dsems = []
        self.live = {"sb": [], "ps": []}
        self.sb_off = SB_BASE
        self.nalloc = 0

    def _register(self, b):
        if b.space is None:
            return
        keep = []
        for o in self.live[b.space]:
            if o.lo < b.hi and b.lo < o.hi:
                o.dead = True
                if o.w is not None:
                    b.r[("w", id(o.w))] = o.w
                for k, v in o.r.items():
                    b.r[(k, id(v))] = v
                if not (b.lo <= o.lo and o.hi <= b.hi):
                    keep.append(o)
            else:
                keep.append(o)
        keep.append(b)
        self.live[b.space] = keep

    def sb(self, name, shape, dt, off=None):
        esz = 2 if dt == BF16 else 4
        n = 1
        for s in shape[1:]:
            n *= s
        nbytes = (n * esz + 63) // 64 * 64
        if off is None:
            off = self.sb_off
            self.sb_off += nbytes
        assert off + nbytes <= SB_LIMIT, (name, off, nbytes)
        self.nalloc += 1
        t = self.nc.alloc_sbuf_tensor_at("%s_%d" % (name, self.nalloc), list(shape), dt, offset=off)
        b = Buf(t, name, "sb", off, off + nbytes)
        self._register(b)
        return b

    def view(self, ap, name, space, lo, hi):
        b = Buf(ap, name, space, lo, hi)
        self._register(b)
        return b

    def op(self, eng, fn, reads=(), writes=(), dma=None):
        dsem = None
        if dma is not None:
            if dma.dsem is None:
                dma.dsem = DSem()
                self.dsems.append(dma.dsem)
            dsem = dma.dsem
        ins = Ins(eng, fn, dsem, self.n)
        self.n += 1
        deps = {}
        for b in reads:
            assert not b.dead, b.name
            if b.w is not None:
                deps[id(b.w)] = b.w
        for b in writes:
            assert not b.dead, b.name
            if b.w is not None:
                deps[id(b.w)] = b.w
            for r in b.r.values():
                deps[id(r)] = r
        ins.deps = list(deps.values())
        key = eng if dsem is None else ("dma", ins.idx)
        for b in reads:
            b.r[key] = ins
        for b in writes:
            b.w = ins
            b.r = {}
        self.q[eng].append(ins)
        return ins

    def emit(self, final_waits=()):
        nc = self.nc
        for e in ENGS:
            for ins in self.q[e]:
                nd = []
                for d in ins.deps:
                    if d is ins:
                        continue
                    if d.dsem is None and ins.dsem is None and d.eng == "pe" and ins.eng == "pe":
                        continue
                    if d.dsem is not None and ins.dsem is d.dsem:
                        continue
                    nd.append(d)
                    d.signal = True
                ins.deps = nd
        for d in final_waits:
            d.signal = True
        cnt = {e: 0 for e in ENGS}
        allins = sorted((i for e in ENGS for i in self.q[e]), key=lambda i: i.idx)
        for ins in allins:
            if ins.dsem is not None:
                ins.dsem.count += 16
                ins.semval = ins.dsem.count
            elif ins.signal:
                cnt[ins.eng] += 1
                ins.semval = cnt[ins.eng]
        with contextlib.ExitStack() as st:
            esem = {e: st.enter_context(nc.semaphore("s_" + e)) for e in ENGS}
            for i, ds in enumerate(self.dsems):
                ds.h = st.enter_context(nc.semaphore("d%d" % i))
            block = st.enter_context(nc.Block())

            def run(e, eh):
                waited = {}
                for ins in self.q[e]:
                    need = {}
                    for d in ins.deps:
                        h = d.dsem.h if d.dsem is not None else esem[d.eng]
                        k = id(h)
                        if need.get(k, (None, 0))[1] < d.semval:
                            need[k] = (h, d.semval)
                    for k, (h, v) in need.items():
                        if waited.get(k, 0) < v:
                            eh.wait_ge(h, v)
                            waited[k] = v
                    r = ins.fn(eh)
                    if ins.dsem is not None:
                        r.then_inc(ins.dsem.h, 16)
                    elif ins.signal:
                        r.then_inc(esem[e], 1)
                if e == "sp":
                    for d in final_waits:
                        h = d.dsem.h if d.dsem is not None else esem[d.eng]
                        eh.wait_ge(h, d.semval)

            block.tensor(lambda eh: run("pe", eh))
            block.vector(lambda eh: run("dve", eh))
            block.scalar(lambda eh: run("act", eh))
            block.gpsimd(lambda eh: run("pool", eh))
            block.sync(lambda eh: run("sp", eh))


def build_program(debug=None, upto=99):
    nc = bass.Bass("TRN2", target_bir_lowering=False)
    P = Prog(nc)

    def din(name, shape, dt=F32):
        return nc.dram_tensor(name, list(shape), dt, kind="ExternalInput").ap()

    def dscr(name, shape, dt=F32):
        return nc.dram_tensor(name, list(shape), dt, kind="Internal").ap()

    x_d = din("x", [S, D])
    ctx_d = din("ctx", [256, D])
    cT_d = din("cT", [128, 8, 2])
    wada_d = din("w_ada", [2, D, 6 * D])
    bada_d = din("b_ada", [2, 6 * D])
    gmix_d = din("g_mix", [2, D])
    gffn_d = din("g_ffn", [2, D])
    gfin_d = din("g_final", [D])
    win_d = din("w_in", [D, 1792])
    wout_d = din("w_out", [D, D])
    gqk_d = din("gqk", [640])
    gv_d = din("gv", [512])
    wsT_d = din("wsT", [128, 8, 128])
    bsT_d = din("bsT", [128, 8])
    wpool_d = din("w_pool", [4, 256, 256])
    pscale_d = din("pool_scale", [D])
    w1_d = din("w1", [2, D, DFF])
    w3_d = din("w3", [2, D, DFF])
    w2_d = din("w2", [2, DFF, D])
    ident_d = din("ident", [128, 128], BF16)
    rope_d = din("rope", [128, NTC, 128])
    band_d = din("band", [128, 4, 5, 128], BF16)
    out_d = nc.dram_tensor("out", [S, D], F32, kind="ExternalOutput").ap()

    vec_d = dscr("vecscr", [2, 2, 9 * D])
    qkind = "ExternalOutput" if (debug and "qt" in debug) else "Internal"
    qt_d = nc.dram_tensor("qt_scr", [NT, 128, 512], BF16, kind=qkind).ap()
    gmt_d = nc.dram_tensor("gmt_scr", [NT, 128, 512], BF16, kind=qkind).ap()
    dram_qt = Buf(None, "qt_scr")
    dram_gmt = Buf(None, "gmt_scr")
    xa_d = nc.dram_tensor("xa", [S, D], F32, kind=("ExternalOutput" if (debug and "x1" in debug) else "Internal")).ap()
    xb_d = dscr("xb", [S, D])
    dbg = {}
    if debug:
        for name, shape in debug.items():
            if not isinstance(shape, (list, tuple)) or len(shape) < 2:
                continue
            dbg[name] = nc.dram_tensor("dbg_" + name, list(shape), F32, kind="ExternalOutput").ap()
    dram_vec = Buf(None, "vecscr")
    dram_xa = Buf(None, "xa")
    dram_xb = Buf(None, "xb")
    finals = []

    psA = nc.alloc_psum_tensor("psA", [128, 2048], F32)
    psB = nc.alloc_psum_tensor("psB", [128, 2048], F32)

    def bank(i, name, ncols=512, dt=F32, nb=1):
        t = psA if i < 4 else psB
        j = i % 4
        ap = t[:, j * 512:(j + nb) * 512]
        if dt == BF16:
            ap = ap.bitcast(BF16)
        return P.view(ap, name, "ps", i, i + nb)

    ident = P.sb("ident", [128, 128], BF16)
    epsT = P.sb("epsT", [128, 16], F32)
    nhT = P.sb("nhT", [128, 16], F32)
    onesF = P.sb("onesF", [128, 64], F32)
    P.op("sp", lambda e: e.dma_start(out=ident[:], in_=ident_d[:, :]), writes=[ident], dma=ident)
    P.op("pool", lambda e: e.memset(epsT[:], EPS), writes=[epsT])
    P.op("pool", lambda e: e.memset(nhT[:], -0.5), writes=[nhT])
    P.op("pool", lambda e: e.memset(onesF[:], 1.0), writes=[onesF])
    negC = P.sb("negC", [128, 1], F32)
    mqk = P.sb("mqk", [128, 2], F32)
    persist_mark = P.sb_off

    def dbg_dump(name, buf, ap_fn, dram_ap_fn):
        if name in dbg:
            i = P.op("sp", lambda e: e.dma_start(out=dram_ap_fn(dbg[name]), in_=ap_fn(buf)), reads=[buf], dma=buf)
            finals.append(i)

    def rstd_pool(ms, rs, n):
        P.op("pool", lambda e: e.tensor_tensor(out=rs[:, 0:n], in0=ms[:, 0:n], in1=epsT[:, 0:n], op=ALU.add),
             reads=[ms, epsT], writes=[rs])
        P.op("pool", lambda e: e.tensor_tensor(out=rs[:, 0:n], in0=rs[:, 0:n], in1=nhT[:, 0:n], op=ALU.pow),
             reads=[rs, nhT], writes=[rs])

    P.sb_off = persist_mark
    cT = P.sb("cT", [128, 8, 2], F32)
    scT = P.sb("scT", [128, 8, 2], F32)
    modrow = [P.sb("modrow%d" % l, [2, 9 * D], F32) for l in range(2)]
    brow = [P.sb("brow%d" % l, [2, 6 * D], F32) for l in range(2)]
    grow = P.sb("grow", [2, 5, D], F32)
    wa = [P.sb("wa%d" % i, [128, 8, 512], F32) for i in range(2)]
    pm0 = [bank(i, "pm0_%d" % i) for i in range(2)]

    P.op("sp", lambda e: e.dma_start(out=cT[:], in_=cT_d[:, :, :]), writes=[cT], dma=cT)
    P.op("act", lambda e: e.activation(out=scT[:], in_=cT[:], func=AF.Silu), reads=[cT], writes=[scT])
    for l in range(2):
        P.op("sp", lambda e, l=l: e.dma_start(out=brow[l][:], in_=bada_d[l].partition_broadcast(2)),
             writes=[brow[l]], dma=brow[l])
    P.op("sp", lambda e: e.dma_start(out=grow[:, 0:2, :], in_=gmix_d.partition_broadcast(2)), writes=[grow], dma=grow)
    P.op("sp", lambda e: e.dma_start(out=grow[:, 2:4, :], in_=gffn_d.partition_broadcast(2)), writes=[grow], dma=grow)
    P.op("sp", lambda e: e.dma_start(out=grow[:, 4, :], in_=pscale_d.partition_broadcast(2)), writes=[grow], dma=grow)
    it = 0
    for l in range(2):
        for cb in range(12):
            w = wa[it % 2]
            pm = pm0[it % 2]
            P.op(os.environ.get("WAQ", "sp" if it % 2 == 0 else "pool") if os.environ.get("WAQ") else ("sp" if it % 2 == 0 else "pool"),
                 lambda e, l=l, cb=cb, w=w: e.dma_start(
                     out=w[:], in_=wada_d[l, :, cb * 512:(cb + 1) * 512].rearrange("(k p) n -> p k n", p=128)),
                 writes=[w], dma=w)
            for k in range(8):
                P.op("pe", lambda e, k=k, w=w, pm=pm: e.matmul(pm[0:2, :], lhsT=scT[:, k, :], rhs=w[:, k, :],
                                                                start=(k == 0), stop=(k == 7)),
                     reads=[scT, w], writes=[pm])
            P.op("dve", lambda e, l=l, cb=cb, pm=pm: e.tensor_tensor(
                out=modrow[l][:, cb * 512:(cb + 1) * 512], in0=pm[0:2, :], in1=brow[l][:, cb * 512:(cb + 1) * 512],
                op=ALU.add), reads=[pm, brow[l]], writes=[modrow[l]])
            it += 1
        P.op("dve", lambda e, l=l: e.scalar_tensor_tensor(
            out=modrow[l][:, 6 * D:7 * D], in0=modrow[l][:, 1 * D:2 * D], scalar=1.0, in1=grow[:, l, :],
            op0=ALU.add, op1=ALU.mult), reads=[modrow[l], grow], writes=[modrow[l]])
        P.op("dve", lambda e, l=l: e.scalar_tensor_tensor(
            out=modrow[l][:, 7 * D:8 * D], in0=modrow[l][:, 4 * D:5 * D], scalar=1.0, in1=grow[:, 2 + l, :],
            op0=ALU.add, op1=ALU.mult), reads=[modrow[l], grow], writes=[modrow[l]])
        P.op("dve", lambda e, l=l: e.tensor_tensor(
            out=modrow[l][:, 8 * D:9 * D], in0=modrow[l][:, 2 * D:3 * D], in1=grow[:, 4, :], op=ALU.mult),
            reads=[modrow[l], grow], writes=[modrow[l]])
        P.op("sp", lambda e, l=l: e.dma_start(out=vec_d[l], in_=modrow[l][:]), reads=[modrow[l]], writes=[dram_vec],
             dma=modrow[l])

    def load_vec(buf, l, row, idx):
        P.op("sp", lambda e: e.dma_start(out=buf[:], in_=vec_d[l, row, idx * D:(idx + 1) * D].partition_broadcast(128)),
             reads=[dram_vec], writes=[buf], dma=buf)

    if "mod" in dbg:
        for l in range(2):
            i = P.op("sp", lambda e, l=l: e.dma_start(out=dbg["mod"][l], in_=modrow[l][:]), reads=[modrow[l]],
                     dma=modrow[l])
            finals.append(i)
    if upto <= 0:
        P.emit(finals)
        return nc

    P.sb_off = persist_mark
    KT = P.sb("KT", [128, NTC * 128], BF16)
    VA = P.sb("VA", [128, NTC + 1, 2, 66], BF16)
    p2_mark = P.sb_off
    A1 = P.sb("A1", [128, D], F32)
    B1 = P.sb("B1", [128, D], F32)
    A1c = P.sb("A1c", [128, D], F32)
    B1c = P.sb("B1c", [128, D], F32)
    load_vec(A1, 0, 0, 6)
    load_vec(B1, 0, 0, 0)
    load_vec(A1c, 0, 1, 6)
    load_vec(B1c, 0, 1, 0)
    win = P.sb("win", [128, 8, 1792], BF16)
    for k in range(8):
        P.op("pool", lambda e, k=k: e.dma_start(out=win[:, k, :], in_=win_d[k * 128:(k + 1) * 128, :]),
             writes=[win], dma=win)
    wsT = P.sb("wsT", [128, 8, 128], BF16)
    P.op("pool", lambda e: e.dma_start(out=wsT[:], in_=wsT_d[:, :, :]), writes=[wsT], dma=wsT)
    bsT = P.sb("bsT", [128, 8], F32)
    P.op("sp", lambda e: e.dma_start(out=bsT[:], in_=bsT_d[:, :]), writes=[bsT], dma=bsT)
    gqk = P.sb("gqk", [128, 640], F32)
    P.op("sp", lambda e: e.dma_start(out=gqk[:], in_=gqk_d.partition_broadcast(128)), writes=[gqk], dma=gqk)
    P.op("dve", lambda e: e.tensor_reduce(out=mqk[:, 0:1], in_=gqk[:, 0:64], axis=AX.X, op=ALU.max,
                                          apply_absolute_value=True), reads=[gqk], writes=[mqk])
    P.op("dve", lambda e: e.tensor_reduce(out=mqk[:, 1:2], in_=gqk[:, 512:576], axis=AX.X, op=ALU.max,
                                          apply_absolute_value=True), reads=[gqk], writes=[mqk])
    P.op("dve", lambda e: e.scalar_tensor_tensor(out=negC[:], in0=mqk[:, 0:1], scalar=-8.0, in1=mqk[:, 1:2],
                                                 op0=ALU.mult, op1=ALU.mult), reads=[mqk], writes=[negC])
    gv = P.sb("gv", [128, 512], F32)
    P.op("sp", lambda e: e.dma_start(out=gv[:], in_=gv_d.partition_broadcast(128)), writes=[gv], dma=gv)
    rope = P.sb("rope", [128, NTC, 128], F32)
    P.op("sp", lambda e: e.dma_start(out=rope[:], in_=rope_d[:, :, :]), writes=[rope], dma=rope)
    P.op("dve", lambda e: e.memset(VA[:, NTC, :, :], 0.0), writes=[VA])
    P.op("dve", lambda e: e.memset(VA[:, 0:NTC, :, 64:66], 1.0), writes=[VA])

    NB = 2
    xt = [P.sb("xt%d" % i, [128, D], F32) for i in range(NB)]
    junk = [P.sb("junk%d" % i, [128, D], BF16) for i in range(2)]
    ms = [P.sb("ms%d" % i, [128, 1], F32) for i in range(NB)]
    rs = [P.sb("rs%d" % i, [128, 1], F32) for i in range(NB)]
    xm = xt
    qts = [P.sb("qts%d" % i, [128, 512], BF16) for i in range(NB)]
    gts = [P.sb("gts%d" % i, [128, 512], BF16) for i in range(NB)]
    xn = [P.sb("xn%d" % i, [128, D], BF16) for i in range(NB)]
    xnT = [P.sb("xnT%d" % i, [128, 8, 128], BF16) for i in range(NB)]
    qsq = [P.sb("qsq%d" % i, [128, 640], F32) for i in range(NB)]
    msq = [P.sb("msq%d" % i, [128, 10], F32) for i in range(NB)]
    rq = [P.sb("rq%d" % i, [128, 10], F32) for i in range(NB)]
    qn = [P.sb("qn%d" % i, [128, 640], F32) for i in range(NB)]
    t1 = [P.sb("t1%d" % i, [128, 640], F32) for i in range(NB)]
    t2 = [P.sb("t2%d" % i, [128, 640], F32) for i in range(NB)]
    qr = [P.sb("qr%d" % i, [128, 640], BF16) for i in range(NB)]
    ug = [P.sb("ug%d" % i, [128, 512], F32) for i in range(NB)]
    vg = [P.sb("vg%d" % i, [128, 512], F32) for i in range(NB)]
    msv = [P.sb("msv%d" % i, [128, 8], F32) for i in range(NB)]
    rv = [P.sb("rv%d" % i, [128, 8], F32) for i in range(NB)]
    vn = [P.sb("vn%d" % i, [128, 512], F32) for i in range(NB)]
    vnb = [P.sb("vnb%d" % i, [128, 512], BF16) for i in range(NB)]
    m1 = [P.sb("m1%d" % i, [128, 512], F32) for i in range(NB)]
    gm = [P.sb("gm%d" % i, [128, 512], BF16) for i in range(NB)]
    pT = bank(0, "pT", dt=BF16)
    pqk = bank(1, "pqk", nb=2)
    pu = bank(3, "pu")
    pg = bank(4, "pg")
    pmx = bank(5, "pmx")
    pT2 = bank(6, "pT2", dt=BF16)
    pT3 = bank(7, "pT3", dt=BF16)

    for t in range(NTC):
        s = t % NB
        isx = t < NT
        src = x_d[t * 128:(t + 1) * 128, :] if isx else ctx_d[(t - NT) * 128:(t - NT + 1) * 128, :]
        Ag, Bg = (A1, B1) if isx else (A1c, B1c)
        P.op("sp", lambda e, s=s, src=src: e.dma_start(out=xt[s][:], in_=src), writes=[xt[s]], dma=xt[s])
        P.op("act", lambda e, s=s: e.activation(out=junk[0][:], in_=xt[s][:], func=AF.Square, scale=1.0 / 32.0,
                                                accum_out=ms[s][:]), reads=[xt[s]], writes=[junk[0], ms[s]])
        rstd_pool(ms[s], rs[s], 1)
        P.op("dve", lambda e, s=s, Ag=Ag: e.scalar_tensor_tensor(out=xm[s][:], in0=xt[s][:], scalar=rs[s][:, 0:1],
                                                                  in1=Ag[:], op0=ALU.mult, op1=ALU.mult),
             reads=[xt[s], rs[s], Ag], writes=[xm[s]])
        P.op("pool", lambda e, s=s, Bg=Bg: e.tensor_tensor(out=xn[s][:], in0=xm[s][:], in1=Bg[:], op=ALU.add),
             reads=[xm[s], Bg], writes=[xn[s]])
        for k in range(8):
            P.op("pe", lambda e, s=s, k=k: e.transpose(out=pT[:, k * 128:(k + 1) * 128],
                                                        in_=xn[s][:, k * 128:(k + 1) * 128], identity=ident[:]),
                 reads=[xn[s], ident], writes=[pT])
        P.op("act", lambda e, s=s: e.copy(out=xnT[s][:].rearrange("p k t -> p (k t)"), in_=pT[:, :]),
             reads=[pT], writes=[xnT[s]])
        blocks = [(pqk, 0, 0, 512), (pqk, 512, 512, 256)]
        if isx:
            blocks += [(pu, 0, 768, 512), (pg, 0, 1280, 512)]
        else:
            blocks = [(pqk, 512, 512, 256)]
        for (pb, po, wo, n) in blocks:
            for k in range(8):
                P.op("pe", lambda e, s=s, k=k, pb=pb, po=po, wo=wo, n=n: e.matmul(
                    pb[:, po:po + n], lhsT=xnT[s][:, k, :], rhs=win[:, k, wo:wo + n], start=(k == 0), stop=(k == 7)),
                    reads=[xnT[s], win], writes=[pb])
        c0 = 0 if isx else 512
        nh = 10 if isx else 2
        h0 = 0 if isx else 8
        w_ = nh * 64
        P.op("act", lambda e, s=s, c0=c0, w_=w_: e.activation(out=qsq[s][:, c0:c0 + w_], in_=pqk[:, c0:c0 + w_],
                                                             func=AF.Square, scale=0.125),
             reads=[pqk], writes=[qsq[s]])
        P.op("dve", lambda e, s=s, c0=c0, w_=w_, nh=nh, h0=h0: e.tensor_reduce(
            out=msq[s][:, h0:h0 + nh], in_=qsq[s][:, c0:c0 + w_].rearrange("p (h d) -> p h d", d=64), axis=AX.X,
            op=ALU.add), reads=[qsq[s]], writes=[msq[s]])
        rstd_pool(msq[s], rq[s], 10)
        P.op("dve", lambda e, s=s, c0=c0, w_=w_, nh=nh, h0=h0: e.tensor_tensor(
            out=qn[s][:, c0:c0 + w_].rearrange("p (h d) -> p h d", d=64),
            in0=pqk[:, c0:c0 + w_].rearrange("p (h d) -> p h d", d=64),
            in1=rq[s][:, h0:h0 + nh].unsqueeze(2).to_broadcast([128, nh, 64]), op=ALU.mult),
            reads=[pqk, rq[s]], writes=[qn[s]])
        P.op("pool", lambda e, s=s, c0=c0, w_=w_: e.tensor_tensor(out=qn[s][:, c0:c0 + w_], in0=qn[s][:, c0:c0 + w_],
                                                                 in1=gqk[:, c0:c0 + w_], op=ALU.mult),
             reads=[qn[s], gqk], writes=[qn[s]])
        cosd = rope[:, t, 0:64]
        sinsg = rope[:, t, 64:128]
        P.op("dve", lambda e, s=s, c0=c0, w_=w_, nh=nh, cosd=cosd: e.tensor_tensor(
            out=t1[s][:, c0:c0 + w_].rearrange("p (h d) -> p h d", d=64),
            in0=qn[s][:, c0:c0 + w_].rearrange("p (h d) -> p h d", d=64),
            in1=cosd.unsqueeze(1).to_broadcast([128, nh, 64]), op=ALU.mult),
            reads=[qn[s], rope], writes=[t1[s]])
        for par in range(2):
            P.op("pool", lambda e, s=s, c0=c0, w_=w_, nh=nh, sinsg=sinsg, par=par: e.tensor_tensor(
                out=t2[s][:, c0:c0 + w_].rearrange("p (h i two) -> p h i two", i=32, two=2)[:, :, :, par],
                in0=qn[s][:, c0:c0 + w_].rearrange("p (h i two) -> p h i two", i=32, two=2)[:, :, :, 1 - par],
                in1=sinsg.rearrange("p (i two) -> p i two", two=2)[:, :, par].unsqueeze(1).to_broadcast([128, nh, 32]),
                op=ALU.mult), reads=[qn[s], rope], writes=[t2[s]])
        P.op("dve", lambda e, s=s, c0=c0, w_=w_: e.tensor_tensor(out=qr[s][:, c0:c0 + w_], in0=t1[s][:, c0:c0 + w_],
                                                                in1=t2[s][:, c0:c0 + w_], op=ALU.add),
             reads=[t1[s], t2[s]], writes=[qr[s]])
        trs = list(range(5)) if isx else [4]
        for j in trs:
            P.op("pe", lambda e, s=s, j=j: e.transpose(out=pT2[:, j * 128:(j + 1) * 128],
                                                        in_=qr[s][:, j * 128:(j + 1) * 128], identity=ident[:]),
                 reads=[qr[s], ident], writes=[pT2])
        if isx:
            P.op("act", lambda e, s=s: e.copy(out=qts[s][:], in_=pT2[:, 0:512]), reads=[pT2], writes=[qts[s]])
            P.op("sp", lambda e, s=s, t=t: e.dma_start(out=qt_d[t], in_=qts[s][:]), reads=[qts[s]], writes=[dram_qt],
                 dma=qts[s])
        P.op("act", lambda e, t=t: e.copy(out=KT[:, t * 128:(t + 1) * 128], in_=pT2[:, 512:640]),
             reads=[pT2], writes=[KT])
        P.op("dve", lambda e, t=t: e.tensor_copy(out=VA[:, t, :, 0:64],
                                                 in_=pqk[:, 640:768].rearrange("p (h d) -> p h d", d=64)),
             reads=[pqk], writes=[VA])
        if not isx:
            continue
        P.op("act", lambda e, s=s: e.activation(out=ug[s][:], in_=pu[:, :], func=AF.Gelu), reads=[pu], writes=[ug[s]])
        P.op("act", lambda e, s=s: e.activation(out=vg[s][:], in_=pg[:, :], func=AF.Gelu), reads=[pg], writes=[vg[s]])
        P.op("act", lambda e, s=s: e.activation(out=junk[1][:, 0:512], in_=vg[s][:], func=AF.Square, scale=0.125),
             reads=[vg[s]], writes=[junk[1]])
        P.op("dve", lambda e, s=s: e.tensor_reduce(out=msv[s][:], in_=junk[1][:, 0:512].rearrange("p (h d) -> p h d", d=64),
                                                   axis=AX.X, op=ALU.add), reads=[junk[1]], writes=[msv[s]])
        rstd_pool(msv[s], rv[s], 8)
        P.op("dve", lambda e, s=s: e.tensor_tensor(
            out=vn[s][:].rearrange("p (h d) -> p h d", d=64), in0=vg[s][:].rearrange("p (h d) -> p h d", d=64),
            in1=rv[s][:, :].unsqueeze(2).to_broadcast([128, 8, 64]), op=ALU.mult), reads=[vg[s], rv[s]], writes=[vn[s]])
        P.op("pool", lambda e, s=s: e.tensor_tensor(out=vnb[s][:], in0=vn[s][:], in1=gv[:], op=ALU.mult),
             reads=[vn[s], gv], writes=[vnb[s]])
        for g in range(8):
            P.op("pe", lambda e, s=s, g=g: e.matmul(pmx[:, g * 64:(g + 1) * 64], lhsT=wsT[:, g, :],
                                                     rhs=vnb[s][:, g * 64:(g + 1) * 64], start=True, stop=True),
                 reads=[wsT, vnb[s]], writes=[pmx])
        P.op("dve", lambda e, s=s: e.tensor_tensor(
            out=m1[s][:].rearrange("p (h d) -> p h d", d=64), in0=pmx[:, :].rearrange("p (h d) -> p h d", d=64),
            in1=bsT[:, :].unsqueeze(2).to_broadcast([128, 8, 64]), op=ALU.add), reads=[pmx, bsT], writes=[m1[s]])
        P.op("pool", lambda e, s=s: e.tensor_tensor(out=gm[s][:], in0=m1[s][:], in1=ug[s][:], op=ALU.mult),
             reads=[m1[s], ug[s]], writes=[gm[s]])
        for j in range(4):
            P.op("pe", lambda e, s=s, j=j: e.transpose(out=pT3[:, j * 128:(j + 1) * 128],
                                                        in_=gm[s][:, j * 128:(j + 1) * 128], identity=ident[:]),
                 reads=[gm[s], ident], writes=[pT3])
        P.op("act", lambda e, s=s: e.copy(out=gts[s][:], in_=pT3[:, 0:512]), reads=[pT3], writes=[gts[s]])
        P.op("sp", lambda e, s=s, t=t: e.dma_start(out=gmt_d[t], in_=gts[s][:]), reads=[gts[s]], writes=[dram_gmt],
             dma=gts[s])

    if "kt" in dbg:
        stg = P.sb("stg", [128, NTC * 132], F32)
        P.op("dve", lambda e: e.tensor_copy(out=stg[:, 0:NTC * 128], in_=KT[:, :]), reads=[KT], writes=[stg])
        finals.append(P.op("sp", lambda e: e.dma_start(out=dbg["kt"][:, :], in_=stg[:, 0:NTC * 128]), reads=[stg], dma=stg))
        P.op("dve", lambda e: e.tensor_copy(out=stg[:, 0:NTC * 132].rearrange("p (c h d) -> p c h d", c=NTC, h=2),
                                            in_=VA[:, 0:NTC, :, :]), reads=[VA], writes=[stg])
        finals.append(P.op("sp", lambda e: e.dma_start(out=dbg["va"][:, :], in_=stg[:, 0:NTC * 132]), reads=[stg], dma=stg))
    if upto <= 1:
        P.emit(finals)
        return nc

    P.sb_off = p2_mark
    G1 = P.sb("G1", [128, D], F32)
    load_vec(G1, 0, 0, 2)
    wo_a = P.sb("wo_a", [64, 8, D], BF16)
    wo_g = P.sb("wo_g", [128, 4, D], BF16)
    P.op("pool", lambda e: e.dma_start(out=wo_a[:], in_=wout_d[0:512, :].rearrange("(h d) n -> d h n", d=64)),
         writes=[wo_a], dma=wo_a)
    P.op("pool", lambda e: e.dma_start(out=wo_g[:], in_=wout_d[512:1024, :].rearrange("(f p) n -> p f n", p=128)),
         writes=[wo_g], dma=wo_g)
    qtile = [P.sb("qtile%d" % i, [128, 4, 512], BF16) for i in range(2)]
    gtile = [P.sb("gtile%d" % i, [128, 4, 512], BF16) for i in range(2)]
    PT = [P.sb("PT%d" % i, [128, 1024], BF16) for i in range(3)]
    AT = [P.sb("AT%d" % i, [64, 8, 512], BF16) for i in range(2)]
    rsb = [P.sb("rsb%d" % i, [128, 512], F32) for i in range(2)]
    bcs = [P.sb("bcs%d" % i, [64, 512], F32) for i in range(2)]
    xr = [P.sb("xr%d" % i, [128, D], F32) for i in range(2)]
    tmpb = [P.sb("tmpb%d" % i, [128, 512], F32) for i in range(2)]
    psS = [bank(0, "psS0", nb=2), bank(2, "psS1", nb=2)]
    psO = [bank(4, "psO0"), bank(5, "psO1")]
    psBc = bank(6, "psBc")
    psW = bank(7, "psW")

    osb = [[P.sb("osb%d%d" % (i, h), [128, 512], F32) for h in range(2)] for i in range(2)]
    VAflat = VA.ap[:, :, :, :].rearrange("p c h d -> p (c h d)")
    pairs = [(qt, j) for qt in range(8) for j in range(4)]
    groups = [(qt, j, ch) for (qt, j) in pairs for ch in range(NTC)]
    deferred = {}

    def defer(i, fn):
        deferred.setdefault(i, []).append(fn)

    def emit_S(i):
        qt, j, ch = groups[i]
        sb_ = psS[i % 2]
        qtl = qtile[qt % 2]
        for half in range(2):
            P.op("pe", lambda e, half=half: e.matmul(
                sb_[:, half * 512:(half + 1) * 512], lhsT=KT[half * 64:(half + 1) * 64, ch * 128:(ch + 1) * 128],
                rhs=qtl[half * 64:(half + 1) * 64, :, j * 128:(j + 1) * 128], start=True, stop=True),
                reads=[KT, qtl], writes=[sb_])

    def emit_exp(i):
        sb_ = psS[i % 2]
        pt = PT[i % 3]
        P.op("act", lambda e: e.activation(out=pt[:], in_=sb_[:, :], func=AF.Exp, bias=negC[:, 0:1], scale=0.125),
             reads=[sb_, negC], writes=[pt])

    def emit_PV(i):
        qt, j, ch = groups[i]
        pt = PT[i % 3]
        for half in range(2):
            po = psO[half]
            base = (ch * 2 + half) * 66
            P.op("pe", lambda e, half=half, po=po, base=base: e.matmul(
                po[:, :], lhsT=VAflat[:, base:base + 128], rhs=pt[:, half * 512:(half + 1) * 512],
                start=(ch == 0), stop=(ch == NTC - 1)), reads=[VA, pt], writes=[po])

    def emit_pair_final(qt, j, i):
        pi = (qt * 4 + j) % 2
        at = AT[qt % 2]
        for half in range(2):
            hp = 2 * j + half
            po = psO[half]
            o_ = osb[pi][half]
            r_ = rsb[half]
            b_ = bcs[half]
            P.op("act", lambda e, po=po, o_=o_: e.copy(out=o_[0:65, :], in_=po[0:65, :]), reads=[po], writes=[o_])
            P.op("dve", lambda e, o_=o_, r_=r_: e.reciprocal(out=r_[64:65, :], in_=o_[64:65, :]), reads=[o_], writes=[r_])

            def pe_part(hp=hp, o_=o_, r_=r_, b_=b_):
                P.op("pe", lambda e: e.matmul(psBc[0:64, :], lhsT=onesF[64:65, 0:64], rhs=r_[64:65, :], start=True,
                                              stop=True), reads=[onesF, r_], writes=[psBc])
                P.op("act", lambda e: e.copy(out=b_[:], in_=psBc[0:64, :]), reads=[psBc], writes=[b_])
                P.op("dve", lambda e: e.tensor_tensor(out=at[:, hp, :], in0=o_[0:64, :], in1=b_[:], op=ALU.mult),
                     reads=[o_, b_], writes=[at])
            defer(i + 2 + half, pe_part)

    def emit_wout_block(qt, st, oh):
        at = AT[qt % 2]
        gtl = gtile[qt % 2]
        t = qt * 4 + st
        x_ = xr[st % 2]
        tb = tmpb[oh]
        if oh == 0:
            P.op("sp", lambda e: e.dma_start(out=x_[:], in_=x_d[t * 128:(t + 1) * 128, :]), writes=[x_], dma=x_)
        for hp in range(8):
            P.op("pe", lambda e, hp=hp: e.matmul(psW[:, :], lhsT=at[:, hp, st * 128:(st + 1) * 128],
                                                  rhs=wo_a[:, hp, oh * 512:(oh + 1) * 512], start=(hp == 0), stop=False),
                 reads=[at, wo_a], writes=[psW])
        for f in range(4):
            P.op("pe", lambda e, f=f: e.matmul(psW[:, :], lhsT=gtl[:, st, f * 128:(f + 1) * 128],
                                                rhs=wo_g[:, f, oh * 512:(oh + 1) * 512], start=False, stop=(f == 3)),
                 reads=[gtl, wo_g], writes=[psW])
        P.op("dve", lambda e: e.tensor_tensor(out=tb[:], in0=psW[:, :], in1=G1[:, oh * 512:(oh + 1) * 512], op=ALU.mult),
             reads=[psW, G1], writes=[tb])
        P.op("pool", lambda e: e.tensor_tensor(out=x_[:, oh * 512:(oh + 1) * 512], in0=x_[:, oh * 512:(oh + 1) * 512],
                                               in1=tb[:], op=ALU.add), reads=[x_, tb], writes=[x_])
        if oh == 1:
            P.op("sp", lambda e: e.dma_start(out=xa_d[t * 128:(t + 1) * 128, :], in_=x_[:]), reads=[x_], writes=[dram_xa],
                 dma=x_)

    NG = len(groups)
    for i in range(NG + 28):
        if i < NG:
            qt, j, ch = groups[i]
            if j == 0 and ch == 0:
                ql, gl = qtile[qt % 2], gtile[qt % 2]
                P.op("sp", lambda e, qt=qt, ql=ql: e.dma_start(out=ql[:], in_=qt_d[qt * 4:(qt + 1) * 4].rearrange("t p c -> p t c")),
                     reads=[dram_qt], writes=[ql], dma=ql)
                P.op("sp", lambda e, qt=qt, gl=gl: e.dma_start(out=gl[:], in_=gmt_d[qt * 4:(qt + 1) * 4].rearrange("t p c -> p t c")),
                     reads=[dram_gmt], writes=[gl], dma=gl)
            if i == 0:
                emit_S(0)
            emit_exp(i)
            if i + 1 < NG:
                emit_S(i + 1)
            emit_PV(i)
            if ch == NTC - 1:
                emit_pair_final(qt, j, i)
                if j == 3:
                    for b_i in range(8):
                        defer(i + 6 + 2 * b_i, (lambda qt=qt, b_i=b_i: emit_wout_block(qt, b_i // 2, b_i % 2)))
        for fn in deferred.pop(i, []):
            fn()
    assert not deferred

    if "x1" in dbg:
        pass
    if upto <= 2:
        P.emit(finals)
        return nc

    def ffn_phase(l, src_d, dram_src, dst_d, dram_dst, final):
        P.sb_off = persist_mark
        w1b = P.sb("w1b", [128, 8, DFF], BF16)
        w3b = P.sb("w3b", [128, 8, DFF], BF16)
        w2b = P.sb("w2b", [128, NF, D], BF16)
        for k in range(8):
            P.op("pool", lambda e, k=k: e.dma_start(out=w1b[:, k, :], in_=w1_d[l, k * 128:(k + 1) * 128, :]),
                 writes=[w1b], dma=w1b)
            P.op("pool", lambda e, k=k: e.dma_start(out=w3b[:, k, :], in_=w3_d[l, k * 128:(k + 1) * 128, :]),
                 writes=[w3b], dma=w3b)
        for h in range(2):
            P.op("pool", lambda e, h=h: e.dma_start(
                out=w2b[:, h * 11:(h + 1) * 11, :],
                in_=w2_d[l, h * 1408:(h + 1) * 1408, :].rearrange("(f p) n -> p f n", p=128)), writes=[w2b], dma=w2b)
        A2 = P.sb("A2", [128, D], F32)
        B2 = P.sb("B2", [128, D], F32)
        G2 = P.sb("G2", [128, D], F32)
        load_vec(A2, l, 0, 7)
        load_vec(B2, l, 0, 3)
        load_vec(G2, l, 0, 5)
        if final:
            GF = P.sb("GF", [128, D], F32)
            P.op("sp", lambda e: e.dma_start(out=GF[:], in_=gfin_d.partition_broadcast(128)), writes=[GF], dma=GF)
        xt_ = [P.sb("fxt%d" % i, [128, D], F32) for i in range(2)]
        xn_ = [P.sb("fxn%d" % i, [128, D], BF16) for i in range(2)]
        fj = P.sb("fjunk", [128, D], BF16)
        ms_ = [P.sb("fms%d" % i, [128, 1], F32) for i in range(2)]
        rs_ = [P.sb("frs%d" % i, [128, 1], F32) for i in range(2)]
        xT = P.sb("fxT", [128, 8, 512], BF16)
        act = P.sb("fact", [128, NF, 512], BF16)
        sgb = [P.sb("fsg%d" % i, [128, 512], BF16) for i in range(2)]
        xr_ = [P.sb("fxr%d" % i, [128, D], F32) for i in range(2)]
        tb_ = [P.sb("ftb%d" % i, [128, 512], F32) for i in range(2)]
        ps13 = [bank(0, "ps13_0", nb=2), bank(2, "ps13_1", nb=2)]
        psY = [bank(4 + i, "psY%d" % i) for i in range(4)]
        for pb in ps13:
            pb.bf = pb.ap.bitcast(BF16)
        NTT = S // 512

        def prologue(T):
            for st in range(4):
                t = T * 4 + st
                s_ = st % 2
                P.op("sp", lambda e, t=t, s_=s_: e.dma_start(out=xt_[s_][:], in_=src_d[t * 128:(t + 1) * 128, :]),
                     reads=[dram_src], writes=[xt_[s_]], dma=xt_[s_])
                P.op("act", lambda e, s_=s_: e.activation(out=fj[:], in_=xt_[s_][:], func=AF.Square, scale=1.0 / 32.0,
                                                         accum_out=ms_[s_][:]), reads=[xt_[s_]], writes=[fj, ms_[s_]])
                rstd_pool(ms_[s_], rs_[s_], 1)
                P.op("dve", lambda e, s_=s_: e.scalar_tensor_tensor(out=xt_[s_][:], in0=xt_[s_][:], scalar=rs_[s_][:, 0:1],
                                                                     in1=A2[:], op0=ALU.mult, op1=ALU.mult),
                     reads=[xt_[s_], rs_[s_], A2], writes=[xt_[s_]])
                P.op("pool", lambda e, s_=s_: e.tensor_tensor(out=xn_[s_][:], in0=xt_[s_][:], in1=B2[:], op=ALU.add),
                     reads=[xt_[s_], B2], writes=[xn_[s_]])
                pb = ps13[st // 2]
                o = (st % 2) * 1024
                for k in range(8):
                    P.op("pe", lambda e, s_=s_, k=k, pb=pb, o=o: e.transpose(
                        out=pb.bf[:, o + k * 128:o + (k + 1) * 128], in_=xn_[s_][:, k * 128:(k + 1) * 128],
                        identity=ident[:]), reads=[xn_[s_], ident], writes=[pb])
                P.op("act", lambda e, st=st, pb=pb, o=o: e.copy(
                    out=xT[:, :, st * 128:(st + 1) * 128], in_=pb.bf[:, o:o + 1024].rearrange("p (k t) -> p k t", k=8)),
                    reads=[pb], writes=[xT])

        def hidden(T):
            for f in range(NF):
                pb = ps13[f % 2]
                for (wb, o) in ((w1b, 0), (w3b, 512)):
                    for k in range(8):
                        P.op("pe", lambda e, wb=wb, o=o, k=k, pb=pb, f=f: e.matmul(
                            pb[:, o:o + 512], lhsT=wb[:, k, f * 128:(f + 1) * 128], rhs=xT[:, k, :],
                            start=(k == 0), stop=(k == 7)), reads=[wb, xT], writes=[pb])
                sg = sgb[f % 2]
                P.op("act", lambda e, pb=pb, sg=sg: e.activation(out=sg[:], in_=pb[:, 0:512], func=AF.Silu),
                     reads=[pb], writes=[sg])
                P.op("dve", lambda e, pb=pb, sg=sg, f=f: e.tensor_tensor(out=act[:, f, :], in0=pb[:, 512:1024], in1=sg[:],
                                                                         op=ALU.mult), reads=[pb, sg], writes=[act])

        ycount = [0]

        def output(T):
            for st in range(4):
                t = T * 4 + st
                x_ = xr_[st % 2]
                P.op("sp", lambda e, t=t, x_=x_: e.dma_start(out=x_[:], in_=src_d[t * 128:(t + 1) * 128, :]),
                     reads=[dram_src], writes=[x_], dma=x_)
                for oh in range(2):
                    py = psY[ycount[0] % 4]
                    ycount[0] += 1
                    tb = tb_[oh]
                    for f in range(NF):
                        P.op("pe", lambda e, f=f, py=py, st=st, oh=oh: e.matmul(
                            py[:, :], lhsT=act[:, f, st * 128:(st + 1) * 128], rhs=w2b[:, f, oh * 512:(oh + 1) * 512],
                            start=(f == 0), stop=(f == NF - 1)), reads=[act, w2b], writes=[py])
                    P.op("dve", lambda e, py=py, tb=tb, oh=oh: e.tensor_tensor(
                        out=tb[:], in0=py[:, :], in1=G2[:, oh * 512:(oh + 1) * 512], op=ALU.mult),
                        reads=[py, G2], writes=[tb])
                    P.op("pool", lambda e, x_=x_, tb=tb, oh=oh: e.tensor_tensor(
                        out=x_[:, oh * 512:(oh + 1) * 512], in0=x_[:, oh * 512:(oh + 1) * 512], in1=tb[:], op=ALU.add),
                        reads=[x_, tb], writes=[x_])
                if final:
                    s_ = st % 2
                    P.op("act", lambda e, x_=x_, s_=s_: e.activation(out=fj[:], in_=x_[:], func=AF.Square, scale=1.0 / 32.0,
                                                                     accum_out=ms_[s_][:]),
                         reads=[x_], writes=[fj, ms_[s_]])
                    rstd_pool(ms_[s_], rs_[s_], 1)
                    P.op("dve", lambda e, x_=x_, s_=s_: e.scalar_tensor_tensor(
                        out=x_[:], in0=x_[:], scalar=rs_[s_][:, 0:1], in1=GF[:], op0=ALU.mult, op1=ALU.mult),
                        reads=[x_, rs_[s_], GF], writes=[x_])
                i_ = P.op("sp", lambda e, t=t, x_=x_: e.dma_start(out=dst_d[t * 128:(t + 1) * 128, :], in_=x_[:]),
                          reads=[x_], writes=[dram_dst], dma=x_)
                if final:
                    finals.append(i_)

        prologue(0)
        for T in range(NTT):
            hidden(T)
            if T + 1 < NTT:
                prologue(T + 1)
            output(T)

    dram_out = Buf(None, "out")
    if upto <= 3:
        ffn_phase(0, xa_d, dram_xa, out_d, dram_out, False)
        finals.append(dram_out.w)
        P.emit(finals)
        return nc
    ffn_phase(0, xa_d, dram_xa, xb_d, dram_xb, False)

    P.sb_off = persist_mark + 3 * 8 * DFF * 2
    A1 = P.sb("pA1", [128, D], F32)
    B1 = P.sb("pB1", [128, D], F32)
    G1p = P.sb("pG1", [128, D], F32)
    load_vec(A1, 1, 0, 6)
    load_vec(B1, 1, 0, 0)
    load_vec(G1p, 1, 0, 8)
    bandb = P.sb("band", [128, 4, 5, 128], BF16)
    P.op("sp", lambda e: e.dma_start(out=bandb[:], in_=band_d[:, :, :, :]), writes=[bandb], dma=bandb)
    wp = P.sb("wp", [128, 4, 2, 256], BF16)
    for g in range(4):
        P.op("pool", lambda e, g=g: e.dma_start(out=wp[:, g, :, :], in_=wpool_d[g].rearrange("(cc p) d -> p cc d", p=128)),
             writes=[wp], dma=wp)
    pxt = [P.sb("pxt%d" % i, [128, D], F32) for i in range(2)]
    pxn = [P.sb("pxn%d" % i, [128, D], BF16) for i in range(4)]
    pj = P.sb("pjunk", [128, D], BF16)
    pms = [P.sb("pms%d" % i, [128, 1], F32) for i in range(2)]
    prs = [P.sb("prs%d" % i, [128, 1], F32) for i in range(2)]
    pTs = [P.sb("pTs%d" % i, [128, 8, 128], BF16) for i in range(2)]
    pxr = [P.sb("pxr%d" % i, [128, D], F32) for i in range(2)]
    ptb = [P.sb("ptb%d" % i, [128, D], F32) for i in range(2)]
    psP = [bank(0, "psP0", nb=2), bank(2, "psP1", nb=2)]
    psYp = [bank(4, "psYp0", nb=2), bank(6, "psYp1", nb=2)]

    def poolA(i):
        s_ = i % 2
        P.op("sp", lambda e: e.dma_start(out=pxt[s_][:], in_=xb_d[i * 128:(i + 1) * 128, :]), reads=[dram_xb],
             writes=[pxt[s_]], dma=pxt[s_])
        P.op("act", lambda e: e.activation(out=pj[:], in_=pxt[s_][:], func=AF.Square, scale=1.0 / 32.0,
                                           accum_out=pms[s_][:]), reads=[pxt[s_]], writes=[pj, pms[s_]])
        rstd_pool(pms[s_], prs[s_], 1)
        P.op("dve", lambda e: e.scalar_tensor_tensor(out=pxt[s_][:], in0=pxt[s_][:], scalar=prs[s_][:, 0:1], in1=A1[:],
                                                     op0=ALU.mult, op1=ALU.mult), reads=[pxt[s_], prs[s_], A1],
             writes=[pxt[s_]])
        P.op("pool", lambda e: e.tensor_tensor(out=pxn[i % 4][:], in0=pxt[s_][:], in1=B1[:], op=ALU.add),
             reads=[pxt[s_], B1], writes=[pxn[i % 4]])

    def poolB(i):
        pp = psP[i % 2]
        py = psYp[i % 2]
        pt_ = pTs[i % 2]
        x_ = pxr[i % 2]
        tb = ptb[i % 2]
        P.op("sp", lambda e: e.dma_start(out=x_[:], in_=xb_d[i * 128:(i + 1) * 128, :]), reads=[dram_xb], writes=[x_],
             dma=x_)
        srcs = []
        if i > 0:
            srcs.append((i - 1, 0))
        srcs.append((i, 3 if i == 0 else (4 if i == NT - 1 else 1)))
        if i < NT - 1:
            srcs.append((i + 1, 2))
        for cc in range(8):
            g = cc // 2
            for n_, (src, kind) in enumerate(srcs):
                xs = pxn[src % 4]
                P.op("pe", lambda e, cc=cc, g=g, xs=xs, kind=kind, n_=n_: e.matmul(
                    pp[:, cc * 128:(cc + 1) * 128], lhsT=xs[:, cc * 128:(cc + 1) * 128], rhs=bandb[:, g, kind, :],
                    start=(n_ == 0), stop=(n_ == len(srcs) - 1)), reads=[xs, bandb], writes=[pp])
        P.op("act", lambda e: e.copy(out=pt_[:].rearrange("p c t -> p (c t)"), in_=pp[:, :]), reads=[pp], writes=[pt_])
        for g in range(4):
            for c2 in range(2):
                P.op("pe", lambda e, g=g, c2=c2: e.matmul(py[:, g * 256:(g + 1) * 256], lhsT=pt_[:, 2 * g + c2, :],
                                                           rhs=wp[:, g, c2, :], start=(c2 == 0), stop=(c2 == 1)),
                     reads=[pt_, wp], writes=[py])
        P.op("dve", lambda e: e.tensor_tensor(out=tb[:], in0=py[:, :], in1=G1p[:], op=ALU.mult), reads=[py, G1p], writes=[tb])
        P.op("pool", lambda e: e.tensor_tensor(out=x_[:], in0=x_[:], in1=tb[:], op=ALU.add), reads=[x_, tb], writes=[x_])
        P.op("sp", lambda e: e.dma_start(out=xa_d[i * 128:(i + 1) * 128, :], in_=x_[:]), reads=[x_], writes=[dram_xa], dma=x_)

    poolA(0)
    for i in range(NT):
        if i + 1 < NT:
            poolA(i + 1)
        poolB(i)
    if upto <= 4:
        P.emit(finals)
        return nc

    ffn_phase(1, xa_d, dram_xa, out_d, dram_out, True)
    P.emit(finals)
    return nc


HEAD_ORDER = [0, 4, 1, 5, 2, 6, 3, 7]


def _consts():
    bf = ml_dtypes.bfloat16
    ident = np.eye(128, dtype=np.float32).astype(bf)
    n = np.arange(S)
    row = (n // 64).astype(np.float64)
    col = (n % 64).astype(np.float64)
    half = 32
    freqs = 10000.0 ** (-np.arange(0, half, 2, dtype=np.float64) / half)
    ang = np.concatenate([row[:, None] * freqs, col[:, None] * freqs], axis=-1).astype(np.float32)
    cos = np.cos(ang).astype(np.float32)
    sin = np.sin(ang).astype(np.float32)
    cosd = np.repeat(cos, 2, axis=1)
    sinsg = np.stack([-sin, sin], axis=-1).reshape(S, 64)
    tab = np.concatenate([cosd, sinsg], axis=1)
    ctab = np.concatenate([np.ones((256, 64), np.float32), np.zeros((256, 64), np.float32)], axis=1)
    tab = np.concatenate([tab, ctab], axis=0).reshape(NTC, 128, 128).transpose(1, 0, 2)
    rope = np.ascontiguousarray(tab, dtype=np.float32)
    band = np.zeros((128, 4, 5, 128), np.float32)
    for g, w in enumerate((2, 4, 8, 16)):
        left = w // 2
        right = w - 1 - left
        for kind in range(5):
            for tt in range(128):
                if kind == 3:
                    T = tt
                elif kind == 4:
                    T = S - 128 + tt
                else:
                    T = 2048 + tt
                lo = max(T - left, 0)
                hi = min(T + right + 1, S)
                cnt = hi - lo
                base = T - tt
                for src in range(lo, hi):
                    rel = src - base
                    if kind == 0 and -128 <= rel < 0:
                        band[rel + 128, g, 0, tt] += 1.0 / cnt
                    elif kind == 2 and 128 <= rel < 256:
                        band[rel - 128, g, 2, tt] += 1.0 / cnt
                    elif kind in (1, 3, 4) and 0 <= rel < 128:
                        band[rel, g, kind, tt] += 1.0 / cnt
                if kind in (1, 3, 4):
                    band[tt, g, kind, tt] -= 1.0
    return ident, rope, band.astype(bf)


def make_in_maps(inp):
    f = lambda a: np.ascontiguousarray(np.asarray(a), dtype=np.float32)
    ident, rope, band = _consts()
    w_in = f(inp["w_in"])[0]
    qcols = np.concatenate([np.arange(h * 64, (h + 1) * 64) for h in HEAD_ORDER])
    w_in_p = np.ascontiguousarray(np.concatenate([w_in[:, qcols], w_in[:, 512:]], axis=1))
    w_out = f(inp["w_out"])[0]
    w_out_p = np.ascontiguousarray(np.concatenate([w_out[qcols, :], w_out[512:, :]], axis=0))
    gqk = np.concatenate([np.tile(f(inp["q_norm"])[0], 8), np.tile(f(inp["k_norm"])[0], 2)])
    gv = f(inp["gmlp_norm"])[0].reshape(512)
    wsT = np.ascontiguousarray(f(inp["w_spatial"])[0].transpose(2, 0, 1))
    bsT = np.ascontiguousarray(f(inp["b_spatial"])[0].T)
    shared = dict(
        w_ada=f(inp["w_ada"]), b_ada=f(inp["b_ada"]), g_mix=f(inp["g_mix"]), g_ffn=f(inp["g_ffn"]),
        g_final=f(inp["g_final"]), w_in=w_in_p, w_out=w_out_p, gqk=np.ascontiguousarray(gqk),
        gv=np.ascontiguousarray(gv), wsT=wsT, bsT=bsT, w_pool=f(inp["w_pool"])[0],
        pool_scale=f(inp["pool_scale"])[0], w1=f(inp["w1"]), w3=f(inp["w3"]), w2=f(inp["w2"]),
        ident=ident, rope=rope, band=band)
    x = f(inp["x"])
    c = f(inp["c"])
    ctx = f(inp["ctx"])
    c_ctx = f(inp["c_ctx"])
    maps = []
    for b in range(8):
        cv = np.stack([c[b], c_ctx], axis=0)
        cT = np.ascontiguousarray(cv.reshape(2, 8, 128).transpose(2, 1, 0))
        m = dict(shared)
        m.update(x=x[b], ctx=ctx[b], cT=cT)
        maps.append(m)
    return maps


def kernel(**inputs):
    nc = build_program()
    maps = make_in_maps(inputs)
    res = run_bass_kernel_spmd(nc, maps, core_ids=list(range(8)))
    return np.stack([np.asarray(r["out"], dtype=np.float32) for r in res.results], axis=0)
```
